# Optimizing a Trainium2 kernel written in Bass

```python
import jax
import jax.numpy as jnp
from jax import lax
import numpy as np


D_MODEL = 1024
BATCH = 4
SEQ = 8192
DEPTH = 2

CHUNK = 64
N_BRANCH = 4
BRANCH_W = 512
CONV_CH = BRANCH_W
CONV_WIDTH = 31
SGU_CH = BRANCH_W
SGU_GROUPS = 4
SGU_WIN = 128
HG_HEADS = 4
HG_DK = 128
HG_DV = BRANCH_W // HG_HEADS
HG_W = HG_HEADS * HG_DK
RW_HEADS = 8
RW_N = 64
RW_W = RW_HEADS * RW_N
RW_DECAY_LORA = 64
RW_AAA_LORA = 64
RW_MV_LORA = 32
RW_GATE_LORA = 160
FFN_HIDDEN = ((8 * D_MODEL + 3 * 256 - 1) // (3 * 256)) * 256
IN_SIZES = (CONV_CH, CONV_CH, SGU_CH, SGU_CH, HG_W, HG_W, HG_HEADS * HG_DV, HG_HEADS * HG_DV, RW_W, RW_W, RW_W)
IN_COLS = sum(IN_SIZES)
RMS_EPS = 1e-6
LN_EPS = 1e-5
RW_GN_EPS = 64e-5

kernel_name = 'gated_hybrid_streaming_encoder'


def _rms_norm(x, g, eps=RMS_EPS):
    xf = x.astype(jnp.float32)
    y = xf * lax.rsqrt(jnp.mean(xf * xf, axis=-1, keepdims=True) + eps)
    return (y * g.astype(jnp.float32)).astype(x.dtype)


def _layer_norm(x, g, b, eps=LN_EPS):
    xf = x.astype(jnp.float32)
    xc = xf - jnp.mean(xf, axis=-1, keepdims=True)
    y = xc * lax.rsqrt(jnp.mean(xc * xc, axis=-1, keepdims=True) + eps)
    return (y * g.astype(jnp.float32) + b.astype(jnp.float32)).astype(x.dtype)


def _token_shift(x):
    return jnp.pad(x, ((0, 0), (1, 0), (0, 0)))[:, :-1]


def _conformer_conv(p_val, p_gate, conv_w, conv_b, ln_g, ln_b):
    a = p_val * jax.nn.sigmoid(p_gate)
    a = lax.conv_general_dilated(
        a, conv_w[:, None, :], window_strides=(1,),
        padding=((CONV_WIDTH - 1, 0),),
        dimension_numbers=('NWC', 'WIO', 'NWC'),
        feature_group_count=CONV_CH) + conv_b
    return jax.nn.silu(_layer_norm(a, ln_g, ln_b))


def _spatial_gating(p_u, p_v, ln_g, ln_b, w_sp, b_sp):
    bsz, seq, _ = p_u.shape
    u = jax.nn.gelu(p_u, approximate=False)
    v = _layer_norm(jax.nn.gelu(p_v, approximate=False), ln_g, ln_b)
    v = v.reshape(bsz, seq // SGU_WIN, SGU_WIN, SGU_GROUPS, SGU_CH // SGU_GROUPS)
    blk = jnp.arange(SGU_WIN) // CHUNK
    allowed = blk[None, :] <= blk[:, None]
    w = jnp.where(allowed[None], w_sp, jnp.zeros_like(w_sp))
    mixed = jnp.einsum('gij,bwjgc->bwigc', w, v) + b_sp.T[:, :, None]
    return u * mixed.reshape(bsz, seq, SGU_CH)


def _hgrn2(p_q, p_f, p_i, p_g, lower_bound, norm_g):
    f32 = jnp.float32
    bsz, seq, _ = p_q.shape
    n_chunks = seq // CHUNK
    q = jax.nn.silu(p_q.astype(f32))
    log_f = jnp.logaddexp(jnp.log(lower_bound), jnp.log1p(-lower_bound) + jax.nn.log_sigmoid(p_f.astype(f32)))
    k = -jnp.expm1(log_f)
    v = p_i.astype(f32)

    def to_chunks(t, d):
        return t.reshape(bsz, n_chunks, CHUNK, HG_HEADS, d).transpose(1, 0, 3, 2, 4)

    causal = jnp.tril(jnp.ones((CHUNK, CHUNK), dtype=bool))[:, :, None]

    def chunk_step(state, inp):
        qc, kc, vc, lc = inp
        b = jnp.cumsum(lc, axis=2)
        diff = b[:, :, :, None, :] - b[:, :, None, :, :]
        decay = jnp.exp(jnp.where(causal, diff, -jnp.inf))
        scores = jnp.einsum('bhtk,bhsk,bhtsk->bhts', qc, kc, decay)
        out = (jnp.einsum('bhts,bhsv->bhtv', scores, vc)
               + jnp.einsum('bhtk,bhkv->bhtv', qc * jnp.exp(b), state))
        b_end = b[:, :, -1:, :]
        state = (jnp.exp(b_end)[:, :, 0, :, None] * state
                 + jnp.einsum('bhsk,bhsv->bhkv', kc * jnp.exp(b_end - b), vc))
        return state, out

    state0 = jnp.zeros((bsz, HG_HEADS, HG_DK, HG_DV), f32)
    _, o = lax.scan(chunk_step, state0,
                    (to_chunks(q, HG_DK), to_chunks(k, HG_DK), to_chunks(v, HG_DV), to_chunks(log_f, HG_DK)))
    o = o.transpose(1, 0, 3, 2, 4).reshape(bsz, seq, HG_HEADS, HG_DV)
    o = o * lax.rsqrt(jnp.mean(o * o, axis=-1, keepdims=True) + RMS_EPS)
    o = o.reshape(bsz, seq, HG_HEADS * HG_DV) * norm_g.astype(f32) * jax.nn.silu(p_g.astype(f32))
    return o.astype(p_q.dtype)


def _rwkv7(h, p_r, p_k, p_v, v_first, mu_rkv, mu_lora, w0, w1, w2, a0, a1, a2,
           g1, g2, k_k, k_a, r_k, ln_g, ln_b, vres):
    f32 = jnp.float32
    bsz, seq, _ = h.shape
    dh = _token_shift(h) - h
    xw = h + dh * mu_lora[0]
    xa = h + dh * mu_lora[1]
    xg = h + dh * mu_lora[2]
    r = p_r + (_token_shift(p_r) - p_r) * mu_rkv[0]
    k = p_k + (_token_shift(p_k) - p_k) * mu_rkv[1]
    v = p_v + (_token_shift(p_v) - p_v) * mu_rkv[2]
    w_log = -jax.nn.softplus(-(w0 + jnp.tanh(xw @ w1) @ w2).astype(f32)) - 0.5
    decay = jnp.exp(-jnp.exp(w_log))
    if vres is None:
        v_first = v
    else:
        mu_v, v0, v1, v2 = vres
        xv = h + dh * mu_v
        v = v + (v_first - v) * jax.nn.sigmoid(v0 + (xv @ v1) @ v2)
    a = jax.nn.sigmoid((a0 + (xa @ a1) @ a2).astype(f32))
    gate = jax.nn.sigmoid(xg @ g1) @ g2
    kf = k.astype(f32)
    kk = (kf * k_k).reshape(bsz, seq, RW_HEADS, RW_N)
    kk = kk * lax.rsqrt(jnp.maximum(jnp.sum(kk * kk, axis=-1, keepdims=True), 1e-24))
    kf = kf * (1.0 + (a - 1.0) * k_a)
    rf = r.astype(f32)
    vf = v.astype(f32)

    def heads(t):
        return t.reshape(bsz, seq, RW_HEADS, RW_N)

    def seq_major(t):
        return heads(t).transpose(1, 0, 2, 3)

    def time_step(state, inp):
        r_t, w_t, k_t, v_t, kk_t, a_t = inp
        s_kk = jnp.einsum('bhvk,bhk->bhv', state, kk_t)
        state = (state * w_t[:, :, None, :]
                 - s_kk[..., None] * (kk_t * a_t)[:, :, None, :]
                 + v_t[..., None] * k_t[:, :, None, :])
        return state, jnp.einsum('bhvk,bhk->bhv', state, r_t)

    state0 = jnp.zeros((bsz, RW_HEADS, RW_N, RW_N), f32)
    _, y = lax.scan(time_step, state0,
                    (seq_major(rf), seq_major(decay), seq_major(kf), seq_major(vf),
                     kk.transpose(1, 0, 2, 3), seq_major(a)))
    y = y.transpose(1, 0, 2, 3)
    yc = y - jnp.mean(y, axis=-1, keepdims=True)
    y = yc * lax.rsqrt(jnp.mean(yc * yc, axis=-1, keepdims=True) + RW_GN_EPS)
    y = y * ln_g.astype(f32).reshape(RW_HEADS, RW_N) + ln_b.astype(f32).reshape(RW_HEADS, RW_N)
    y = y + jnp.sum(heads(rf) * heads(kf) * r_k.astype(f32), axis=-1, keepdims=True) * heads(vf)
    y = y.reshape(bsz, seq, RW_W) * gate.astype(f32)
    return y.astype(h.dtype), v_first


def setup_inputs(seed: int = 0) -> dict:
    key = jax.random.key(seed)
    ks = iter(jax.random.split(key, 48))
    D = D_MODEL

    def nrm(shape, scale):
        return scale * jax.random.normal(next(ks), shape, jnp.float32)

    def uni(shape, lo, hi):
        return jax.random.uniform(next(ks), shape, jnp.float32, lo, hi)

    def gain(shape):
        return 1.0 + nrm(shape, 0.05)

    return {
        'x': nrm((BATCH, SEQ, D), 1.0),
        'norm_mix_g': gain((DEPTH, D)),
        'w_in': nrm((DEPTH, D, IN_COLS), D ** -0.5),
        'w_gate': nrm((DEPTH, N_BRANCH, D, D), D ** -0.5),
        'b_gate': nrm((DEPTH, N_BRANCH, D), 0.1),
        'conv_w': nrm((DEPTH, CONV_WIDTH, CONV_CH), CONV_WIDTH ** -0.5),
        'conv_b': nrm((DEPTH, CONV_CH), 0.02),
        'conv_ln_g': gain((DEPTH, CONV_CH)),
        'conv_ln_b': nrm((DEPTH, CONV_CH), 0.02),
        'sgu_ln_g': gain((DEPTH, SGU_CH)),
        'sgu_ln_b': nrm((DEPTH, SGU_CH), 0.02),
        'sgu_w': nrm((DEPTH, SGU_GROUPS, SGU_WIN, SGU_WIN), SGU_WIN ** -0.5),
        'sgu_b': 1.0 + nrm((DEPTH, SGU_GROUPS, SGU_WIN), 0.05),
        'hg_lb': nrm((DEPTH, HG_W), 1.0),
        'hg_norm_g': gain((DEPTH, HG_HEADS * HG_DV)),
        'rw_mu_rkv': uni((DEPTH, 3, RW_W), 0.0, 1.0),
        'rw_mu_lora': uni((DEPTH, 3, D), 0.0, 1.0),
        'rw_w0': uni((DEPTH, RW_W), -6.0, 0.0),
        'rw_w1': nrm((DEPTH, D, RW_DECAY_LORA), D ** -0.5),
        'rw_w2': nrm((DEPTH, RW_DECAY_LORA, RW_W), RW_DECAY_LORA ** -0.5),
        'rw_a0': nrm((DEPTH, RW_W), 0.1),
        'rw_a1': nrm((DEPTH, D, RW_AAA_LORA), D ** -0.5),
        'rw_a2': nrm((DEPTH, RW_AAA_LORA, RW_W), RW_AAA_LORA ** -0.5),
        'rw_g1': nrm((DEPTH, D, RW_GATE_LORA), D ** -0.5),
        'rw_g2': nrm((DEPTH, RW_GATE_LORA, RW_W), RW_GATE_LORA ** -0.5),
        'rw_k_k': 0.85 + nrm((DEPTH, RW_W), 0.05),
        'rw_k_a': 1.0 + nrm((DEPTH, RW_W), 0.05),
        'rw_r_k': nrm((DEPTH, RW_HEADS, RW_N), 0.1),
        'rw_ln_g': gain((DEPTH, RW_W)),
        'rw_ln_b': nrm((DEPTH, RW_W), 0.02),
        'rw_mu_vres': uni((DEPTH - 1, D), 0.0, 1.0),
        'rw_v0': 1.0 + nrm((DEPTH - 1, RW_W), 0.1),
        'rw_v1': nrm((DEPTH - 1, D, RW_MV_LORA), D ** -0.5),
        'rw_v2': nrm((DEPTH - 1, RW_MV_LORA, RW_W), RW_MV_LORA ** -0.5),
        'w_branch': nrm((DEPTH, N_BRANCH, BRANCH_W, D), BRANCH_W ** -0.5),
        'w_out': nrm((DEPTH, D, D), D ** -0.5),
        'norm_ffn_g': gain((DEPTH, D)),
        'w_ffn_gate': nrm((DEPTH, D, FFN_HIDDEN), D ** -0.5),
        'w_ffn_up': nrm((DEPTH, D, FFN_HIDDEN), D ** -0.5),
        'w_ffn_down': nrm((DEPTH, FFN_HIDDEN, D), FFN_HIDDEN ** -0.5),
        'final_norm_g': gain((D,)),
    }


def reference(x, norm_mix_g, w_in, w_gate, b_gate, conv_w, conv_b, conv_ln_g, conv_ln_b,
              sgu_ln_g, sgu_ln_b, sgu_w, sgu_b, hg_lb, hg_norm_g,
              rw_mu_rkv, rw_mu_lora, rw_w0, rw_w1, rw_w2, rw_a0, rw_a1, rw_a2,
              rw_g1, rw_g2, rw_k_k, rw_k_a, rw_r_k, rw_ln_g, rw_ln_b,
              rw_mu_vres, rw_v0, rw_v1, rw_v2,
              w_branch, w_out, norm_ffn_g, w_ffn_gate, w_ffn_up, w_ffn_down, final_norm_g):
    offsets = [int(o) for o in np.cumsum(IN_SIZES)[:-1]]
    lbs = jnp.cumsum(jax.nn.softmax(hg_lb.astype(jnp.float32), axis=0), axis=0)
    lbs = lbs - lbs[0]
    v_first = None
    for l in range(DEPTH):
        h = _rms_norm(x, norm_mix_g[l])
        p = h @ w_in[l]
        (pa_val, pa_gate, pb_u, pb_v, pc_q, pc_f, pc_i, pc_g,
         pd_r, pd_k, pd_v) = jnp.split(p, offsets, axis=-1)
        y_a = _conformer_conv(pa_val, pa_gate, conv_w[l], conv_b[l], conv_ln_g[l], conv_ln_b[l])
        y_b = _spatial_gating(pb_u, pb_v, sgu_ln_g[l], sgu_ln_b[l], sgu_w[l], sgu_b[l])
        y_c = _hgrn2(pc_q, pc_f, pc_i, pc_g, lbs[l], hg_norm_g[l])
        vres = None if l == 0 else (rw_mu_vres[l - 1], rw_v0[l - 1], rw_v1[l - 1], rw_v2[l - 1])
        y_d, v_first = _rwkv7(h, pd_r, pd_k, pd_v, v_first, rw_mu_rkv[l], rw_mu_lora[l],
                              rw_w0[l], rw_w1[l], rw_w2[l], rw_a0[l], rw_a1[l], rw_a2[l],
                              rw_g1[l], rw_g2[l], rw_k_k[l], rw_k_a[l], rw_r_k[l],
                              rw_ln_g[l], rw_ln_b[l], vres)
        mixed = jnp.zeros_like(x)
        for j, y_j in enumerate((y_a, y_b, y_c, y_d)):
            g_j = jax.nn.sigmoid(h @ w_gate[l, j] + b_gate[l, j])
            mixed = mixed + g_j * (y_j @ w_branch[l, j])
        x = x + mixed @ w_out[l]
        h = _rms_norm(x, norm_ffn_g[l])
        x = x + (jax.nn.silu(h @ w_ffn_gate[l]) * (h @ w_ffn_up[l])) @ w_ffn_down[l]
    return _rms_norm(x, final_norm_g)
```

```python
import numpy as np
from contextlib import ExitStack
import concourse.bass as bass
import concourse.mybir as mybir
from concourse.bass_utils import run_bass_kernel_spmd

F32 = mybir.dt.float32
BF16 = mybir.dt.bfloat16
AF = mybir.ActivationFunctionType
ALU = mybir.AluOpType
AX = mybir.AxisListType

D = 1024
KC = 8
T = 256
J = T // 128
INC = 5632
FFH = 2816
NROW = 14 * 512
LPP = 228
CONVW = 31
HIST = 30


class Buf:
    def __init__(self, t, name, leaves):
        self.t, self.name, self.leaves = t, name, leaves

    def __getitem__(self, idx):
        return self.t[idx]

    def h(self, *ls):
        return [(self.name, l) for l in (ls if ls else range(self.leaves))]


class KB:
    ENG = ('pe', 'act', 'dve', 'pool', 'sp')

    def __init__(self, nc, es):
        self.nc, self.es = nc, es
        self.prog = {e: [] for e in self.ENG}
        self.cnt = {}
        self.waited = {e: {} for e in self.ENG}
        self.lastw = {}
        self.readers = {}
        self.semh = {}
        self.dma_sems = set()

    def sb(self, name, shape, dt, leaves=1):
        return Buf(self.es.enter_context(self.nc.sbuf_tensor("S_" + name, list(shape), dt)), name, leaves)

    def ps(self, name, shape, dt=F32, leaves=1):
        return Buf(self.es.enter_context(self.nc.psum_tensor("P_" + name, list(shape), dt)), name, leaves)

    def op(self, eng, fn, reads=(), writes=(), dsem=None):
        need = {}

        def add(d):
            if d is not None:
                need[d[0]] = max(need.get(d[0], 0), d[1])
        for h in reads:
            add(self.lastw.get(h))
        for h in writes:
            add(self.lastw.get(h))
            for d in self.readers.get(h, {}).items():
                add(d)
        if eng == 'pe':
            need.pop('pe', None)
        waits = []
        for k, v in need.items():
            if self.waited[eng].get(k, 0) < v:
                self.waited[eng][k] = v
                waits.append((k, v))
        if dsem is None:
            self.cnt[eng] = self.cnt.get(eng, 0) + 1
            me = (eng, self.cnt[eng])
        else:
            self.dma_sems.add(dsem)
            self.cnt[dsem] = self.cnt.get(dsem, 0) + 16
            me = (dsem, self.cnt[dsem])
        self.prog[eng].append((waits, fn, me))
        for h in reads:
            r = self.readers.setdefault(h, {})
            r[me[0]] = max(r.get(me[0], 0), me[1])
        for h in writes:
            self.lastw[h] = me
            self.readers[h] = {}
        return me

    def emit(self, final_waits):
        keys = set()
        waited = {}
        for e in self.ENG:
            for waits, fn, me in self.prog[e]:
                keys.add(me[0])
                for k, v in waits:
                    keys.add(k)
                    waited.setdefault(k, set()).add(v)
        rank = {}
        for k, vs in waited.items():
            if k not in self.dma_sems:
                rank[k] = {v: i + 1 for i, v in enumerate(sorted(vs))}
        for k in sorted(keys):
            self.semh[k] = self.es.enter_context(self.nc.semaphore("s_" + k))
        block = self.es.enter_context(self.nc.Block())

        def mk(eng):
            def body(e):
                for waits, fn, me in self.prog[eng]:
                    for k, v in waits:
                        e.wait_ge(self.semh[k], v if k in self.dma_sems else rank[k][v])
                    ins = fn(e)
                    if me[0] in self.dma_sems:
                        ins.then_inc(self.semh[me[0]], 16)
                    elif me[1] in rank.get(me[0], {}):
                        ins.then_inc(self.semh[me[0]], 1)
                if eng == 'sp':
                    for k in final_waits:
                        e.wait_ge(self.semh[k], self.cnt[k])
            return body
        block.tensor(mk('pe'))
        block.scalar(mk('act'))
        block.vector(mk('dve'))
        block.gpsimd(mk('pool'))
        block.sync(mk('sp'))


def build(SEQ, dbg=None):
    NT = SEQ // T
    nc = bass.Bass("TRN2", target_bir_lowering=False)
    es = ExitStack()
    kb = KB(nc, es)
    op = kb.op

    def din(name, shape, dt=F32):
        return nc.dram_tensor(name, list(shape), dt, kind="ExternalInput").ap()

    x_d = din("x", [SEQ, D])
    out_d = nc.dram_tensor("out", [SEQ, D], F32, kind="ExternalOutput").ap()
    w_in_d = din("w_in", [2, D, INC])
    w_gate_d = din("w_gate", [2, D, 4 * D])
    w_br_d = din("w_branch", [2, 4, 512, D])
    w_out_d = din("w_out", [2, D, D])
    w_fg_d = din("w_fg", [2, D, FFH])
    w_fu_d = din("w_fu", [2, D, FFH])
    w_fd_d = din("w_fd", [2, FFH, D])
    cpp_d = din("cpp", [128, 2 * LPP])
    crow_d = din("crow", [2, NROW])
    frow_d = din("frow", [1, D])
    lbrow_d = din("lbrow", [2, 512])
    spw_d = din("spw", [2, 128, 4, 128])
    l1_d = din("l1", [2, D, 320])
    l2_d = din("l2", [2, 128, 5, 512])

    def dscr(name, shape):
        return nc.dram_tensor(name, list(shape), BF16).ap()
    b_in = dscr("b_in", [2, D, INC])
    b_gate = dscr("b_gate", [2, D, 4 * D])
    b_br = dscr("b_br", [2, 4, 512, D])
    b_out = dscr("b_out", [2, D, D])
    b_fg = dscr("b_fg", [2, D, FFH])
    b_fu = dscr("b_fu", [2, D, FFH])
    b_fd = dscr("b_fd", [2, FFH, D])
    b_l1 = dscr("b_l1", [2, D, 320])
    b_l2 = dscr("b_l2", [2, 128, 5, 512])

    dbg_out = {}
    if dbg:
        for name, shp in dbg.items():
            dbg_out[name] = nc.dram_tensor("dbg_" + name, list(shp), F32, kind="ExternalOutput").ap()

    sb, ps = kb.sb, kb.ps

    xres = sb("xres", [128, J, D], F32, J)
    hT = sb("hT", [128, KC, T + 1], BF16)
    hcar = [sb(f"hcar{l}", [128, KC], BF16) for l in range(2)]
    NSLOT = 4
    wring = [sb(f"wr{i}", [128, KC, 512], BF16) for i in range(NSLOT)]
    cpp = sb("cpp", [128, 2 * LPP], F32)
    crow = sb("crow", [128, NROW], F32)
    omlbtm = sb("omlbtm", [128, 512], F32)
    lbfm = sb("lbfm", [128, 8], F32)
    spw = [sb(f"spw{l}", [128, 4, 128], BF16) for l in range(2)]
    l1w = sb("l1w", [128, KC, 320], BF16)
    l2w = sb("l2w", [128, 5, 512], BF16)
    yT = [sb(f"yT{b}", [128, 4, T], BF16, 1) for b in range(4)]
    mixT = sb("mixT", [128, KC, T], BF16, KC)
    vfirst = sb("vfirst", [128, J, 512], F32, J)
    aT = [sb(f"aT{l}", [128, 4, HIST + T], F32, 4) for l in range(2)]
    FB = [sb(f"FB{i}", [128, 4, T], F32, 4) for i in range(3)]
    Shg = [sb(f"Shg{l}", [128, 4, 128], F32, 4) for l in range(2)]
    Srw = [sb(f"Srw{l}", [128, 4, 128], F32, 4) for l in range(2)]
    s1 = {m: sb("s1" + m, [128, T], BF16) for m in ("w", "a", "g0", "g1", "v")}
    NBIG = 12
    BIG = sb("BIG", [128, NBIG * 512], F32, NBIG)

    class View:
        def __init__(self, ap, blocks):
            self.ap, self.blocks = ap, blocks

        def __getitem__(self, idx):
            return self.ap[idx]

        def h(self, *a):
            return BIG.h(*self.blocks)

    def Dv(i):
        return View(BIG[:, i * 512:(i + 1) * 512], [i])
    Dn = [Dv(i) for i in range(NBIG)]
    frow = View(BIG[:, 0:1024], [0, 1])
    xn = View(BIG[:, 1024:2048], [2, 3])
    dhT = View(BIG[:, 2048:3072].bitcast(BF16).rearrange("p (a b) -> p a b", a=KC), [4, 5])
    xl = View(BIG[:, 3072:4096].bitcast(BF16).rearrange("p (a b) -> p a b", a=KC), [6, 7])
    lbrow = View(BIG[:, 4096:5120].rearrange("p (a b) -> p a b", a=2), [8, 9])
    hid_all = BIG[:, 0:22 * T // 2].bitcast(BF16).rearrange("p (a b) -> p a b", a=22)

    class Hid:
        def __getitem__(self, idx):
            return hid_all[idx]

        def h(self, *hks):
            if not hks:
                hks = range(22)
            return BIG.h(*sorted(set((hk * T * 2) // 2048 for hk in hks)))
    hidT = Hid()
    ident = sb("ident", [128, 128], F32)
    onesf = sb("onesf", [128, 128], F32)
    uincl = sb("uincl", [128, 128], F32)
    ublk = sb("ublk", [128, 128], F32)
    bones = sb("bones", [128, 128], F32)
    ind = sb("ind", [128, 4], F32)
    mk4 = sb("mk4", [128, 4, 128], F32)
    nlstr = sb("nlstr", [128, 128], F32)
    NTMP = 8
    tmps = [sb(f"tmp{i}", [128, 512], F32) for i in range(NTMP)]
    tmpi = [0]

    def tmp():
        b = tmps[tmpi[0] % NTMP]
        tmpi[0] += 1
        return b
    NSM = 12
    smalls = [sb(f"sm{i}", [128, 16], F32) for i in range(NSM)]
    smi = [0]

    def small():
        b = smalls[smi[0] % NSM]
        smi[0] += 1
        return b
    KR = sb("KR", [128, 4, 2, 128], F32)
    KtT = sb("KtT", [128, 4, 128], F32)
    BtT = sb("BtT", [128, 4, 128], F32)
    E1 = [sb(f"E1_{i}", [128, 4, 128], F32) for i in range(2)]
    XB = [sb(f"XB{p}", [128, 3, 128], F32) for p in range(2)]
    MT = sb("MT", [128, 128], F32)
    akv = sb("akv", [128, 64], F32)
    U1p = [sb(f"U1p{i}", [128, 128], F32, 2) for i in range(2)]
    W1T = [sb(f"W1T{i}", [128, 128], F32, 2) for i in range(2)]
    Ut = [sb(f"Ut{i}", [128, 128], F32) for i in range(2)]
    NPS = 6
    psb = [ps(f"ps{i}", [128, 512], F32) for i in range(NPS)]
    psL = [ps(f"psL{i}", [128, 512], F32) for i in range(2)]
    psi = [0]

    def pst():
        b = psb[psi[0] % NPS]
        psi[0] += 1
        return b

    def dma(eng, out_ap, in_ap, reads, writes, dsem):
        return op(eng, lambda e: e.dma_start(out=out_ap, in_=in_ap), reads, writes, dsem)

    def mm(out_ap, lhsT, rhs, start, stop, reads, writes):
        return op('pe', lambda e: e.matmul(out_ap, lhsT, rhs, start=start, stop=stop), reads, writes)

    def tr(out_ap, in_ap, reads, writes):
        return op('pe', lambda e: e.transpose(out_ap, in_ap, ident[:]), reads + ident.h(), writes)

    def act(out_ap, in_ap, func, reads, writes, scale=1.0, bias=0.0, accum=None, eng='act'):
        if accum is None:
            return op(eng, lambda e: e.activation(out=out_ap, in_=in_ap, func=func, bias=bias, scale=scale), reads, writes)
        return op(eng, lambda e: e.activation(out=out_ap, in_=in_ap, func=func, bias=bias, scale=scale, accum_out=accum), reads, writes)

    def tt(eng, out_ap, a, b, alu, reads, writes):
        return op(eng, lambda e: e.tensor_tensor(out=out_ap, in0=a, in1=b, op=alu), reads, writes)

    def ts(eng, out_ap, a, s1_, s2_, op0, op1, reads, writes):
        if op1 is None:
            return op(eng, lambda e: e.tensor_scalar(out=out_ap, in0=a, scalar1=s1_, scalar2=None, op0=op0), reads, writes)
        return op(eng, lambda e: e.tensor_scalar(out=out_ap, in0=a, scalar1=s1_, scalar2=s2_, op0=op0, op1=op1), reads, writes)

    def stt(eng, out_ap, a, s, b, op0, op1, reads, writes):
        return op(eng, lambda e: e.scalar_tensor_tensor(out=out_ap, in0=a, scalar=s, in1=b, op0=op0, op1=op1), reads, writes)

    def cp(eng, out_ap, in_ap, reads, writes):
        if eng == 'act':
            return act(out_ap, in_ap, AF.Copy, reads, writes)
        return op(eng, lambda e: e.tensor_copy(out=out_ap, in_=in_ap), reads, writes)

    def rsum(eng, out_ap, in_ap, reads, writes):
        return op(eng, lambda e: e.reduce_sum(out=out_ap, in_=in_ap, axis=AX.X), reads, writes)

    def recip(out_ap, in_ap, reads, writes):
        return op('dve', lambda e: e.reciprocal(out=out_ap, in_=in_ap), reads, writes)

    def mset(eng, ap, val, writes):
        return op(eng, lambda e: e.memset(ap, val), (), writes)

    def asel(out_ap, in_ap, pattern, cmp_op, fill, base, cm, reads, writes):
        return op('pool', lambda e: e.affine_select(out=out_ap, in_=in_ap, pattern=pattern, compare_op=cmp_op,
                                                    fill=fill, base=base, channel_multiplier=cm), reads, writes)

    def dump(name, ap, reads):
        if name in dbg_out:
            dma('pool', dbg_out[name], ap, reads, [("dbg", name)], "sg_dbg_" + name)

    mset('pool', ident[:], 0.0, ident.h())
    asel(ident[:], ident[:], [[-1, 128]], ALU.not_equal, 1.0, 0, 1, ident.h(), ident.h())
    mset('pool', onesf[:], 1.0, onesf.h())
    mset('pool', uincl[:], 1.0, uincl.h())
    asel(uincl[:], uincl[:], [[1, 128]], ALU.is_ge, 0.0, 0, -1, uincl.h(), uincl.h())
    mset('pool', bones[:], 1.0, bones.h())
    b3 = bones[:].rearrange("p (a b) -> p a b", a=4)
    asel(b3, b3, [[-32, 4], [0, 32]], ALU.is_ge, 0.0, 0, 1, bones.h(), bones.h())
    asel(b3, b3, [[32, 4], [0, 32]], ALU.is_ge, 0.0, 31, -1, bones.h(), bones.h())
    tt('pool', ublk[:], uincl[:], bones[:], ALU.mult, uincl.h() + bones.h(), ublk.h())
    mset('pool', ind[:], 1.0, ind.h())
    asel(ind[:], ind[:], [[-32, 4]], ALU.is_ge, 0.0, 0, 1, ind.h(), ind.h())
    asel(ind[:], ind[:], [[32, 4]], ALU.is_ge, 0.0, 31, -1, ind.h(), ind.h())
    mset('pool', mk4[:], 1.0, mk4.h())
    asel(mk4[:, 0, :], mk4[:, 0, :], [[1, 128]], ALU.is_ge, 0.0, -1, -1, mk4.h(), mk4.h())
    cp('pool', mk4[:, 1, :], uincl[:], uincl.h() + mk4.h(), mk4.h())
    ts('pool', mk4[:, 2, :], mk4[:, 0, :], -1.0, None, ALU.mult, None, mk4.h(), mk4.h())
    cp('pool', mk4[:, 3, :], uincl[:], uincl.h() + mk4.h(), mk4.h())
    mset('pool', nlstr[:], -1.0, nlstr.h())
    asel(nlstr[:], nlstr[:], [[-1, 128]], ALU.is_ge, 0.0, -1, 1, nlstr.h(), nlstr.h())

    dma('sp', cpp[:], cpp_d, [], cpp.h(), "sd_c")
    dma('sp', BIG[:, 4096:5120], lbrow_d.rearrange("a b -> (a b)").partition_broadcast(128),
        [], lbrow.h(), "sd_c")
    for l in range(2):
        dma('pool', spw[l][:], spw_d[l], [], spw[l].h(), "sg_c")
    for hh in spw[0].h() + spw[1].h():
        kb.lastw[hh] = ("sg_c", kb.cnt["sg_c"])
    for hh in cpp.h() + lbrow.h():
        kb.lastw[hh] = ("sd_c", kb.cnt["sd_c"])
    WH = [("dram", "w")]

    def cast(dst, src):
        dma('pool', dst, src, [], WH, "sg_w")
    for l in range(2):
        cast(b_l2[l], l2_d[l])
        for r in range(KC):
            rs = slice(r * 128, (r + 1) * 128)
            cast(b_l1[l, rs, :], l1_d[l, rs, :])
            cast(b_in[l, rs, :], w_in_d[l, rs, :])
            cast(b_gate[l, rs, :], w_gate_d[l, rs, :])
            cast(b_out[l, rs, :], w_out_d[l, rs, :])
            cast(b_fg[l, rs, :], w_fg_d[l, rs, :])
            cast(b_fu[l, rs, :], w_fu_d[l, rs, :])
        for r in range(22):
            rs = slice(r * 128, (r + 1) * 128)
            cast(b_fd[l, rs, :], w_fd_d[l, rs, :])
        for jb in range(4):
            for r in range(4):
                rs = slice(r * 128, (r + 1) * 128)
                cast(b_br[l, jb, rs, :], w_br_d[l, jb, rs, :])
    kb.lastw[WH[0]] = ("sg_w", kb.cnt["sg_w"])

    for l in range(2):
        mset('pool', spw[l][64:128, :, 0:64], 0.0, spw[l].h())
    lbtm = tmp()
    tt('dve', lbtm[:], lbrow[:, 1, :], lbrow[:, 0, :], ALU.subtract, lbrow.h(), lbtm.h())
    act(lbtm[:], lbtm[:], AF.Sigmoid, lbtm.h(), lbtm.h())
    ts('dve', omlbtm[:], lbtm[:], -1.0, 1.0, ALU.mult, ALU.add, lbtm.h(), omlbtm.h())
    tt('dve', lbfm[:, 0:4], cpp[:, 224:228], cpp[:, 220:224], ALU.subtract, cpp.h(), lbfm.h())
    act(lbfm[:, 0:4], lbfm[:, 0:4], AF.Sigmoid, lbfm.h(), lbfm.h())
    ts('dve', lbfm[:, 4:8], lbfm[:, 0:4], -1.0, 1.0, ALU.mult, ALU.add, lbfm.h(), lbfm.h())
    for l in range(2):
        mset('pool', hcar[l][:], 0.0, hcar[l].h())
        mset('pool', aT[l][:, :, 0:HIST], 0.0, aT[l].h())
        mset('pool', Shg[l][:], 0.0, Shg[l].h())
        mset('pool', Srw[l][:], 0.0, Srw[l].h())

    slot_i = [0]

    def wload(src2d, r0, nkc, c0, ncols):
        i = slot_i[0] % NSLOT
        slot_i[0] += 1
        s = wring[i]
        src = src2d[r0 * 128:(r0 + nkc) * 128, c0:c0 + ncols].rearrange("(kc p) n -> p kc n", p=128)
        dma('sp', s[:, 0:nkc, 0:ncols], src, WH, s.h(), f"sd_w{i}")
        return s

    def fm_group(pt, s, cc, rhsbuf, nkc, rd, ncol=T, roff=0):
        for kc in range(nkc):
            mm(pt[:, 0:ncol], s[:, kc, cc * 128:(cc + 1) * 128], rhsbuf[:, kc, roff:roff + ncol],
               kc == 0, kc == nkc - 1, s.h() + rd, pt.h())

    def tm_group(pt, lbuf, loff, s, nkc, rd, ncols=512, first=True, last=True, kc0=0):
        for kc in range(nkc):
            mm(pt[:, 0:ncols], lbuf[:, kc0 + kc, loff:loff + 128], s[:, kc, 0:ncols],
               first and kc == 0, last and kc == nkc - 1, s.h() + rd, pt.h())

    def rstd_from(ssum_ap, scale, eps, rd):
        n = ssum_ap.shape[-1]
        o = small()
        act(o[:, 0:n], ssum_ap, AF.Sqrt, rd, o.h(), scale=scale, bias=eps)
        recip(o[:, 0:n], o[:, 0:n], o.h(), o.h())
        return o

    def rmsnorm_to_hT(gcol0):
        for j in range(J):
            ss = small()
            act(xn[:], xres[:, j, :], AF.Square, xres.h(j), xn.h() + ss.h(), accum=ss[:, 0:1])
            r = rstd_from(ss[:, 0:1], 1.0 / D, 1e-6, ss.h())
            ts('dve', xn[:], xres[:, j, :], r[:, 0:1], None, ALU.mult, None, xres.h(j) + r.h(), xn.h())
            for half in range(2):
                pt = pst()
                for q in range(4):
                    kc = half * 4 + q
                    tr(pt[:, q * 128:(q + 1) * 128], xn[:, kc * 128:(kc + 1) * 128], xn.h(), pt.h())
                g3 = cpp[:, gcol0 + half * 4:gcol0 + half * 4 + 4].unsqueeze(2).broadcast_to([128, 4, 128])
                tt('dve', hT[:, half * 4:half * 4 + 4, 1 + j * 128:1 + (j + 1) * 128],
                   pt[:].rearrange("p (a b) -> p a b", a=4), g3, ALU.mult, pt.h() + cpp.h(), hT.h())

    for ti in range(NT):
        t0 = ti * T
        dma('sp', xres[:], x_d[t0:t0 + T, :].rearrange("(j p) d -> p j d", p=128), [], xres.h(), "sd_x")
        for l in range(2):
            first_tile = (ti == 0)
            P0 = l * LPP
            win = b_in[l]
            dma('sp', crow[:], crow_d[l].partition_broadcast(128), [], crow.h(), "sd_cr")
            dma('sp', l1w[:], b_l1[l].rearrange("(kc p) n -> p kc n", p=128), WH, l1w.h(), "sd_l1")
            dma('sp', l2w[:], b_l2[l], WH, l2w.h(), "sd_l2")

            def cr(i):
                return crow[:, i * 512:(i + 1) * 512]
            rmsnorm_to_hT(P0 + 0)
            cp('pool', hT[:, :, 0], hcar[l][:], hcar[l].h() + hT.h(), hT.h())
            cp('pool', hcar[l][:], hT[:, :, T], hT.h() + hcar[l].h(), hcar[l].h())
            tt('pool', dhT[:], hT[:, :, 0:T], hT[:, :, 1:T + 1], ALU.subtract, hT.h(), dhT.h())
            loras = [("w", 16, 0, 64, AF.Tanh), ("a", 24, 64, 64, AF.Copy), ("g", 32, 128, 160, AF.Sigmoid)]
            if l == 1:
                loras.append(("v", 40, 288, 32, AF.Copy))
            for (m, mucol, woff, R, fn) in loras:
                mu3 = cpp[:, P0 + mucol:P0 + mucol + 8].unsqueeze(2).broadcast_to([128, KC, T])
                tt('pool', xl[:], dhT[:], mu3, ALU.mult, dhT.h() + cpp.h(), xl.h())
                tt('pool', xl[:], xl[:], hT[:, :, 1:T + 1], ALU.add, xl.h() + hT.h(), xl.h())
                parts = [(m, woff, R)] if m != "g" else [("g0", woff, 128), ("g1", woff + 128, 32)]
                for (nm, wo, rr) in parts:
                    pt = pst()
                    for kc in range(KC):
                        mm(pt[0:rr, 0:T], l1w[:, kc, wo:wo + rr], xl[:, kc, :], kc == 0, kc == KC - 1,
                           l1w.h() + xl.h(), pt.h())
                    act(s1[nm][0:rr, :], pt[0:rr, 0:T], fn, pt.h(), s1[nm].h())

            s_val = wload(win, 0, KC, 0, 512)
            s_gate = wload(win, 0, KC, 512, 512)
            for cc in range(4):
                pv, pg = pst(), pst()
                fm_group(pv, s_val, cc, hT, KC, hT.h(), roff=1)
                fm_group(pg, s_gate, cc, hT, KC, hT.h(), roff=1)
                sg = tmp()
                act(sg[:, 0:T], pg[:, 0:T], AF.Sigmoid, pg.h(), sg.h())
                tt('dve', aT[l][:, cc, HIST:HIST + T], pv[:, 0:T], sg[:, 0:T], ALU.mult, pv.h() + sg.h(), aT[l].h(cc))
            for cc in range(4):
                eng = 'dve' if cc % 2 == 0 else 'pool'
                cw0 = P0 + 80 + cc * CONVW
                ts(eng, FB[0][:, cc, :], aT[l][:, cc, 0:T], cpp[:, cw0:cw0 + 1], cpp[:, P0 + 204 + cc:P0 + 205 + cc],
                   ALU.mult, ALU.add, aT[l].h(cc) + cpp.h(), FB[0].h(cc))
            for tap in range(1, CONVW):
                for cc in range(4):
                    eng = 'dve'
                    cw0 = P0 + 80 + cc * CONVW + tap
                    stt(eng, FB[0][:, cc, :], aT[l][:, cc, tap:tap + T], cpp[:, cw0:cw0 + 1], FB[0][:, cc, :],
                        ALU.mult, ALU.add, aT[l].h(cc) + cpp.h() + FB[0].h(cc), FB[0].h(cc))
            for cc in range(4):
                cp('pool', aT[l][:, cc, 0:HIST], aT[l][:, cc, T:T + HIST], aT[l].h(cc), aT[l].h(cc))
            if ti == 0 and l == 0:
                dump("convT", FB[0][:].rearrange("p a b -> p (a b)"), FB[0].h())
            psum_, psq_ = pst(), pst()
            for cc in range(4):
                act(FB[1][:, cc, :], FB[0][:, cc, :], AF.Square, FB[0].h(cc), FB[1].h(cc))
            for cc in range(4):
                mm(psum_[:, 0:T], onesf[:], FB[0][:, cc, :], cc == 0, cc == 3, onesf.h() + FB[0].h(cc), psum_.h())
            for cc in range(4):
                mm(psq_[:, 0:T], onesf[:], FB[1][:, cc, :], cc == 0, cc == 3, onesf.h() + FB[1].h(cc), psq_.h())
            mean = tmp()
            act(mean[:, 0:T], psum_[:, 0:T], AF.Copy, psum_.h(), mean.h(), scale=1.0 / 512)
            msq = tmp()
            tt('dve', msq[:, 0:T], mean[:, 0:T], mean[:, 0:T], ALU.mult, mean.h(), msq.h())
            var = tmp()
            stt('dve', var[:, 0:T], psq_[:, 0:T], 1.0 / 512, msq[:, 0:T], ALU.mult, ALU.subtract,
                psq_.h() + msq.h(), var.h())
            act(var[:, 0:T], var[:, 0:T], AF.Sqrt, var.h(), var.h(), bias=1e-5)
            recip(var[:, 0:T], var[:, 0:T], var.h(), var.h())
            for cc in range(4):
                t1 = tmp()
                tt('dve', t1[:, 0:T], FB[0][:, cc, :], mean[:, 0:T], ALU.subtract, FB[0].h(cc) + mean.h(), t1.h())
                tt('pool', t1[:, 0:T], t1[:, 0:T], var[:, 0:T], ALU.mult, t1.h() + var.h(), t1.h())
                act(yT[0][:, cc, :], t1[:, 0:T], AF.Silu, t1.h() + cpp.h(), yT[0].h(),
                    scale=cpp[:, P0 + 208 + cc:P0 + 209 + cc], bias=cpp[:, P0 + 212 + cc:P0 + 213 + cc])

            s_u = wload(win, 0, KC, 1024, 512)
            s_v = wload(win, 0, KC, 1536, 512)
            for cc in range(4):
                pu = pst()
                fm_group(pu, s_u, cc, hT, KC, hT.h(), roff=1)
                act(FB[2][:, cc, :], pu[:, 0:T], AF.Gelu, pu.h(), FB[2].h(cc))
            for j in range(J):
                pv = pst()
                tm_group(pv, hT, 1 + j * 128, s_v, KC, hT.h())
                gv = tmp()
                sm = small()
                act(gv[:], pv[:], AF.Gelu, pv.h(), gv.h())
                rsum('dve', sm[:, 0:1], gv[:], gv.h(), sm.h())
                jk = tmp()
                act(jk[:], gv[:], AF.Square, gv.h(), jk.h() + sm.h(), accum=sm[:, 1:2])
                ts('dve', sm[:, 2:3], sm[:, 0:1], 1.0 / 512, None, ALU.mult, None, sm.h(), sm.h())
                tt('dve', sm[:, 3:4], sm[:, 2:3], sm[:, 2:3], ALU.mult, sm.h(), sm.h())
                stt('dve', sm[:, 4:5], sm[:, 1:2], 1.0 / 512, sm[:, 3:4], ALU.mult, ALU.subtract, sm.h(), sm.h())
                r = rstd_from(sm[:, 4:5], 1.0, 1e-5, sm.h())
                ts('dve', gv[:], gv[:], sm[:, 2:3], r[:, 0:1], ALU.subtract, ALU.mult, gv.h() + sm.h() + r.h(), gv.h())
                tt('pool', gv[:], gv[:], cr(0), ALU.mult, gv.h() + crow.h(), gv.h())
                vtm = tmp()
                vb = vtm[:].bitcast(BF16)[:, 0:512]
                tt('pool', vb, gv[:], cr(1), ALU.add, gv.h() + crow.h(), vtm.h())
                pm = pst()
                for g in range(4):
                    mm(pm[:, g * 128:(g + 1) * 128], vb[:, g * 128:(g + 1) * 128], spw[l][:, g, :], True, True,
                       vtm.h() + spw[l].h(), pm.h())
                mb = tmp()
                tt('dve', mb[:], pm[:], cr(13), ALU.add, pm.h() + crow.h(), mb.h())
                tt('pool', yT[1][:, :, j * 128:(j + 1) * 128], mb[:].rearrange("p (a b) -> p a b", a=4),
                   FB[2][:, :, j * 128:(j + 1) * 128], ALU.mult, mb.h() + FB[2].h(), yT[1].h())

            qT, kT, gT = FB[0], FB[1], FB[2]
            s_q = wload(win, 0, KC, 2048, 512)
            s_g = wload(win, 0, KC, 3584, 512)
            for hc in range(4):
                pq, pg = pst(), pst()
                fm_group(pq, s_q, hc, hT, KC, hT.h(), roff=1)
                fm_group(pg, s_g, hc, hT, KC, hT.h(), roff=1)
                act(qT[:, hc, :], pq[:, 0:T], AF.Silu, pq.h(), qT.h(hc))
                act(gT[:, hc, :], pg[:, 0:T], AF.Silu, pg.h(), gT.h(hc))
            s_f = wload(win, 0, KC, 2560, 512)
            s_i = wload(win, 0, KC, 3072, 512)
            for hc in range(4):
                pf = pst()
                fm_group(pf, s_f, hc, hT, KC, hT.h(), roff=1)
                act(kT[:, hc, :], pf[:, 0:T], AF.Sigmoid, pf.h(), kT.h(hc), scale=-1.0)
                if l == 1:
                    ts('pool', kT[:, hc, :], kT[:, hc, :], lbfm[:, 4 + hc:5 + hc], None, ALU.mult, None,
                       kT.h(hc) + lbfm.h(), kT.h(hc))
            for j in range(J):
                js = slice(j * 128, (j + 1) * 128)
                ktm, logf, vtm_, qtl, ktl, atm, btm, dd = Dn[0:8]
                khc = Dn[8:12]
                pf, pi = pst(), pst()
                tm_group(pf, hT, 1 + j * 128, s_f, KC, hT.h())
                tm_group(pi, hT, 1 + j * 128, s_i, KC, hT.h())
                act(ktm[:], pf[:], AF.Sigmoid, pf.h(), ktm.h(), scale=-1.0)
                if l == 1:
                    tt('pool', ktm[:], ktm[:], omlbtm[:], ALU.mult, ktm.h() + omlbtm.h(), ktm.h())
                act(logf[:], ktm[:], AF.Ln, ktm.h(), logf.h(), scale=-1.0, bias=1.0)
                cp('dve', vtm_[:], pi[:], pi.h(), vtm_.h())
                pbT, pbTM, pbeTM, pbeT = pst(), pst(), pst(), pst()
                for hc in range(4):
                    mm(pbT[:, hc * 128:(hc + 1) * 128], logf[:, hc * 128:(hc + 1) * 128], ublk[:], True, True,
                       logf.h() + ublk.h(), pbT.h())
                    mm(pbeT[:, hc * 4:hc * 4 + 4], logf[:, hc * 128:(hc + 1) * 128], ind[:], True, True,
                       logf.h() + ind.h(), pbeT.h())
                mm(pbTM[:], ublk[:], logf[:], True, True, logf.h() + ublk.h(), pbTM.h())
                mm(pbeTM[:], bones[:], logf[:], True, True, logf.h() + bones.h(), pbeTM.h())
                eb, enb = tmp(), tmp()
                act(eb[:], pbT[:], AF.Exp, pbT.h(), eb.h())
                act(enb[:], pbT[:], AF.Exp, pbT.h(), enb.h(), scale=-1.0)
                tt('dve', qtl[:].rearrange("p (a b) -> p a b", a=4), qT[:, :, js], eb[:].rearrange("p (a b) -> p a b", a=4),
                   ALU.mult, qT.h() + eb.h(), qtl.h())
                tt('pool', ktl[:].rearrange("p (a b) -> p a b", a=4), kT[:, :, js], enb[:].rearrange("p (a b) -> p a b", a=4),
                   ALU.mult, kT.h() + enb.h(), ktl.h())
                cp('act', btm[:], pbTM[:], pbTM.h(), btm.h())
                tt('dve', dd[:], pbeTM[:], btm[:], ALU.subtract, pbeTM.h() + btm.h(), dd.h())
                act(dd[:], dd[:], AF.Exp, dd.h(), dd.h())
                tt('pool', dd[:], dd[:], ktm[:], ALU.mult, dd.h() + ktm.h(), dd.h())
                for c in range(4):
                    ts('pool', khc[c][:], dd[:], ind[:, c:c + 1], None, ALU.mult, None, dd.h() + ind.h(), khc[c].h())
                ebend = small()
                act(ebend[:, 0:16], pbeT[:, 0:16], AF.Exp, pbeT.h(), ebend.h())
                pA = pst()
                for hc in range(4):
                    hs = slice(hc * 128, (hc + 1) * 128)
                    mm(pA[:, hs], ktl[:, hs], qtl[:, hs], True, True, ktl.h() + qtl.h(), pA.h())
                tt('dve', atm[:].rearrange("p (a b) -> p a b", a=4), pA[:].rearrange("p (a b) -> p a b", a=4),
                   ublk[:].unsqueeze(1).broadcast_to([128, 4, 128]), ALU.mult, pA.h() + ublk.h(), atm.h())
                po = psL[0]
                for hc in range(4):
                    hs = slice(hc * 128, (hc + 1) * 128)
                    for c in range(4):
                        cs = slice(hc * 128 + c * 32, hc * 128 + (c + 1) * 32)
                        mm(po[:, cs], Shg[l][:, hc, :], qtl[:, cs], True, False, Shg[l].h(hc) + qtl.h(), po.h())
                        mm(po[:, cs], vtm_[:, hs], atm[:, cs], False, True, vtm_.h() + atm.h(), po.h())
                        pS = pst()
                        mm(pS[:, 0:128], khc[c][:, hs], vtm_[:, hs], True, True, khc[c].h() + vtm_.h(), pS.h())
                        stt('dve', Shg[l][:, hc, :], Shg[l][:, hc, :], ebend[:, hc * 4 + c:hc * 4 + c + 1], pS[:, 0:128],
                            ALU.mult, ALU.add, Shg[l].h(hc) + ebend.h() + pS.h(), Shg[l].h(hc))
                sq = tmp()
                act(sq[:], po[:], AF.Square, po.h(), sq.h())
                pss = pst()
                mm(pss[:], onesf[:], sq[:], True, True, onesf.h() + sq.h(), pss.h())
                rr = tmp()
                act(rr[:], pss[:], AF.Sqrt, pss.h(), rr.h(), scale=1.0 / 128, bias=1e-6)
                recip(rr[:], rr[:], rr.h(), rr.h())
                tt('dve', rr[:], po[:], rr[:], ALU.mult, po.h() + rr.h(), rr.h())
                tt('pool', rr[:].rearrange("p (a b) -> p a b", a=4), rr[:].rearrange("p (a b) -> p a b", a=4),
                   gT[:, :, js], ALU.mult, rr.h() + gT.h(), rr.h())
                tt('pool', yT[2][:, :, js], rr[:].rearrange("p (a b) -> p a b", a=4),
                   cpp[:, P0 + 216:P0 + 220].unsqueeze(2).broadcast_to([128, 4, 128]), ALU.mult,
                   rr.h() + cpp.h(), yT[2].h())

            s_rkv = [wload(win, 0, KC, 4096 + 512 * m, 512) for m in range(3)]
            for j in range(J):
                js = slice(j * 128, (j + 1) * 128)
                r_tm, v_tm, k_tm, logw, a_tm, gate, kap, kf, beta, cw, Kp = Dn[0:11]
                Rt, Bh, Kt, Bt, Kh = k_tm, logw, a_tm, kap, cw
                for m, o in enumerate((r_tm, k_tm, v_tm)):
                    pp, pshf = pst(), pst()
                    tm_group(pp, hT, 1 + j * 128, s_rkv[m], KC, hT.h())
                    tm_group(pshf, hT, j * 128, s_rkv[m], KC, hT.h())
                    psb_ = tmp()
                    cp('act', psb_[:], pp[:], pp.h(), psb_.h())
                    tt('dve', o[:], pshf[:], psb_[:], ALU.subtract, pshf.h() + psb_.h(), o.h())
                    tt('pool', o[:], o[:], cr(2 + m), ALU.mult, o.h() + crow.h(), o.h())
                    tt('pool', o[:], o[:], psb_[:], ALU.add, o.h() + psb_.h(), o.h())
                pw = pst()
                mm(pw[:], s1["w"][0:64, js], l2w[0:64, 0, :], True, True, s1["w"].h() + l2w.h(), pw.h())
                tt('dve', logw[:], pw[:], cr(5), ALU.add, pw.h() + crow.h(), logw.h())
                act(logw[:], logw[:], AF.Sigmoid, logw.h(), logw.h())
                ts('pool', logw[:], logw[:], -0.6065306597126334, None, ALU.mult, None, logw.h(), logw.h())
                pa = pst()
                mm(pa[:], s1["a"][0:64, js], l2w[0:64, 1, :], True, True, s1["a"].h() + l2w.h(), pa.h())
                tt('dve', a_tm[:], pa[:], cr(6), ALU.add, pa.h() + crow.h(), a_tm.h())
                act(a_tm[:], a_tm[:], AF.Sigmoid, a_tm.h(), a_tm.h())
                pgt = pst()
                mm(pgt[:], s1["g0"][:, js], l2w[:, 2, :], True, False, s1["g0"].h() + l2w.h(), pgt.h())
                mm(pgt[:], s1["g1"][0:32, js], l2w[0:32, 3, :], False, True, s1["g1"].h() + l2w.h(), pgt.h())
                cp('act', gate[:], pgt[:], pgt.h(), gate.h())
                if l == 0:
                    cp('pool', vfirst[:, j, :], v_tm[:], v_tm.h(), vfirst.h(j))
                else:
                    pvv = pst()
                    mm(pvv[:], s1["v"][0:32, js], l2w[0:32, 4, :], True, True, s1["v"].h() + l2w.h(), pvv.h())
                    sv = tmp()
                    tt('dve', sv[:], pvv[:], cr(7), ALU.add, pvv.h() + crow.h(), sv.h())
                    act(sv[:], sv[:], AF.Sigmoid, sv.h(), sv.h())
                    dv = tmp()
                    tt('pool', dv[:], vfirst[:, j, :], v_tm[:], ALU.subtract, vfirst.h(j) + v_tm.h(), dv.h())
                    tt('pool', dv[:], dv[:], sv[:], ALU.mult, dv.h() + sv.h(), dv.h())
                    tt('pool', v_tm[:], v_tm[:], dv[:], ALU.add, v_tm.h() + dv.h(), v_tm.h())
                tt('pool', kap[:], k_tm[:], cr(8), ALU.mult, k_tm.h() + crow.h(), kap.h())
                sqk = tmp()
                act(sqk[:], kap[:], AF.Square, kap.h(), sqk.h())
                sm = small()
                rsum('dve', sm[:, 0:8], sqk[:].rearrange("p (a b) -> p a b", a=8), sqk.h(), sm.h())
                ts('dve', sm[:, 0:8], sm[:, 0:8], 1e-24, None, ALU.max, None, sm.h(), sm.h())
                rn = rstd_from(sm[:, 0:8], 1.0, 0.0, sm.h())
                tt('dve', kap[:].rearrange("p (a b) -> p a b", a=8), kap[:].rearrange("p (a b) -> p a b", a=8),
                   rn[:, 0:8].unsqueeze(2).broadcast_to([128, 8, 64]), ALU.mult, kap.h() + rn.h(), kap.h())
                ts('pool', kf[:], a_tm[:], -1.0, None, ALU.add, None, a_tm.h(), kf.h())
                tt('pool', kf[:], kf[:], cr(9), ALU.mult, kf.h() + crow.h(), kf.h())
                ts('pool', kf[:], kf[:], 1.0, None, ALU.add, None, kf.h(), kf.h())
                tt('pool', kf[:], kf[:], k_tm[:], ALU.mult, kf.h() + k_tm.h(), kf.h())
                tt('pool', beta[:], kap[:], a_tm[:], ALU.mult, kap.h() + a_tm.h(), beta.h())
                pcw, pcwC, pgc = pst(), pst(), pst()
                mm(pcw[:], uincl[:], logw[:], True, True, uincl.h() + logw.h(), pcw.h())
                mm(pcwC[:], onesf[:], logw[:], True, True, onesf.h() + logw.h(), pcwC.h())
                for hp in range(4):
                    mm(pgc[:, hp:hp + 1], logw[:, hp * 128:(hp + 1) * 128], onesf[:, 0:1], True, True,
                       logw.h() + onesf.h(), pgc.h())
                gcT = small()
                act(gcT[:, 0:4], pgc[:, 0:4], AF.Exp, pgc.h(), gcT.h())
                cp('act', cw[:], pcw[:], pcw.h(), cw.h())
                G = tmp()
                act(G[:], cw[:], AF.Exp, cw.h(), G.h())
                tt('pool', Rt[:], r_tm[:], G[:], ALU.mult, r_tm.h() + G.h(), Rt.h())
                Gx = tmp()
                tt('dve', Gx[:], cw[:], logw[:], ALU.subtract, cw.h() + logw.h(), Gx.h())
                act(Gx[:], Gx[:], AF.Exp, Gx.h(), Gx.h())
                tt('pool', Kp[:], kap[:], Gx[:], ALU.mult, kap.h() + Gx.h(), Kp.h())
                Gi = tmp()
                act(Gi[:], cw[:], AF.Exp, cw.h(), Gi.h(), scale=-1.0)
                tt('pool', Kt[:], kf[:], Gi[:], ALU.mult, kf.h() + Gi.h(), Kt.h())
                tt('pool', Bt[:], beta[:], Gi[:], ALU.mult, beta.h() + Gi.h(), Bt.h())
                GCr = tmp()
                tt('dve', GCr[:], pcwC[:], cw[:], ALU.subtract, pcwC.h() + cw.h(), GCr.h())
                act(GCr[:], GCr[:], AF.Exp, GCr.h(), GCr.h())
                tt('pool', Bh[:], beta[:], GCr[:], ALU.mult, beta.h() + GCr.h(), Bh.h())
                tt('pool', Kh[:], kf[:], GCr[:], ALU.mult, kf.h() + GCr.h(), Kh.h())
                for ii, (src, dst3, dh_) in enumerate(((Kp, KR[:, :, 0, :], KR), (Rt, KR[:, :, 1, :], KR),
                                                       (Kt, KtT[:], KtT), (Bt, BtT[:], BtT))):
                    pt = pst()
                    for hp in range(4):
                        tr(pt[:, hp * 128:(hp + 1) * 128], src[:, hp * 128:(hp + 1) * 128], src.h(), pt.h())
                    cp('act' if ii % 2 == 0 else 'dve', dst3, pt[:].rearrange("p (a b) -> p a b", a=4), pt.h(), dh_.h())
                py = psL[1]
                for hp in range(4):
                    for e_ in range(2):
                        h = hp * 2 + e_
                        rows = slice(e_ * 64, (e_ + 1) * 64)
                        hcol = slice(h * 64, (h + 1) * 64)
                        p12, p3 = pst(), pst()
                        KRh = KR[rows, hp, :, :].rearrange("p a b -> p (a b)")
                        mm(p12[:, 0:256], KtT[rows, hp, :], KRh, True, True, KtT.h() + KR.h(), p12.h())
                        mm(p12[:, 256:512], BtT[rows, hp, :], KRh, True, True, BtT.h() + KR.h(), p12.h())
                        mm(p3[:, 0:128], KR[rows, hp, 0, :], BtT[rows, hp, :], True, True, KR.h() + BtT.h(), p3.h())
                        e1 = E1[e_]
                        tt('dve', e1[:], p12[:].rearrange("p (a b) -> p a b", a=4), mk4[:], ALU.mult,
                           p12.h() + mk4.h(), e1.h())
                        xb = XB
                        tt('dve', xb[0][:, 0, :], p3[:, 0:128], nlstr[:], ALU.mult, p3.h() + nlstr.h(), xb[0].h())
                        cp('pool', xb[0][:, 1, :], e1[:, 2, :], e1.h() + xb[0].h(), xb[0].h())
                        cp('pool', xb[0][:, 2, :], ident[:], ident.h() + xb[0].h(), xb[0].h())
                        cur = 0
                        for s in range(1, 7):
                            c_, n_ = xb[cur], xb[1 - cur]
                            pq_ = pst()
                            mm(pq_[:, 0:128], c_[:, 1, :], c_[:, 0, :], True, True, c_.h(), pq_.h())
                            mm(pq_[:, 128:256], c_[:, 0, :], c_[:, 1, :], True, True, c_.h(), pq_.h())
                            mm(pq_[:, 256:384], ident[:], c_[:, 2, :], True, False, c_.h() + ident.h(), pq_.h())
                            mm(pq_[:, 256:384], c_[:, 0, :], c_[:, 2, :], False, True, c_.h(), pq_.h())
                            cp('act' if s % 2 else 'dve', n_[:].rearrange("p a b -> p (a b)"), pq_[:, 0:384],
                               pq_.h(), n_.h())
                            cur = 1 - cur
                        c_ = xb[cur]
                        pq_ = pst()
                        mm(pq_[:, 0:128], ident[:], c_[:, 2, :], True, False, c_.h() + ident.h(), pq_.h())
                        mm(pq_[:, 0:128], c_[:, 0, :], c_[:, 2, :], False, True, c_.h(), pq_.h())
                        mt = MT
                        cp('act', mt[:], pq_[:, 0:128], pq_.h(), mt.h())
                        pk = pst()
                        mm(pk[:, 0:64], e1[:, 0, :], v_tm[:, hcol], True, True, e1.h() + v_tm.h(), pk.h())
                        ak = akv
                        cp('dve', ak[:], pk[:, 0:64], pk.h(), ak.h())
                        mm(pk[:, 128:192], mt[:], ak[:], True, True, mt.h() + ak.h(), pk.h())
                        u1 = U1p[hp % 2]
                        cp('act', u1[:, rows.start:rows.stop], pk[:, 128:192], pk.h(), u1.h(e_))
                        mm(pk[:, 256:384], Kp[:, hp * 128:(hp + 1) * 128], mt[:], True, True, Kp.h() + mt.h(), pk.h())
                        w1 = W1T[hp % 2]
                        cp('dve', w1[rows, :], pk[rows, 256:384], pk.h(), w1.h(e_))
                    u1, w1, ut = U1p[hp % 2], W1T[hp % 2], Ut[hp % 2]
                    pu = pst()
                    mm(pu[:, 0:128], w1[:], Srw[l][:, hp, :], True, True, w1.h() + Srw[l].h(hp), pu.h())
                    stt('dve', ut[:], pu[:, 0:128], -1.0, u1[:], ALU.mult, ALU.subtract, pu.h() + u1.h(), ut.h())
                    for e_ in range(2):
                        h = hp * 2 + e_
                        hcol = slice(h * 64, (h + 1) * 64)
                        ecol = slice(e_ * 64, (e_ + 1) * 64)
                        e1 = E1[e_]
                        mm(py[:, hcol], KR[:, hp, 1, :], Srw[l][:, hp, ecol], True, False, KR.h() + Srw[l].h(hp), py.h())
                        mm(py[:, hcol], e1[:, 3, :], ut[:, ecol], False, False, e1.h() + ut.h(), py.h())
                        mm(py[:, hcol], e1[:, 1, :], v_tm[:, hcol], False, True, e1.h() + v_tm.h(), py.h())
                    pS = pst()
                    hps = slice(hp * 128, (hp + 1) * 128)
                    mm(pS[:, 0:128], Bh[:, hps], ut[:], True, False, Bh.h() + ut.h(), pS.h())
                    mm(pS[:, 0:128], Kh[:, hps], v_tm[:, hps], False, True, Kh.h() + v_tm.h(), pS.h())
                    for e_ in range(2):
                        rows = slice(e_ * 64, (e_ + 1) * 64)
                        stt('dve', Srw[l][rows, hp, rows], Srw[l][rows, hp, rows], gcT[rows, hp:hp + 1], pS[rows, rows],
                            ALU.mult, ALU.add, Srw[l].h(hp) + gcT.h() + pS.h(), Srw[l].h(hp))
                sm = small()
                rsum('dve', sm[:, 0:8], py[:].rearrange("p (a b) -> p a b", a=8), py.h(), sm.h())
                sqy = tmp()
                act(sqy[:], py[:], AF.Square, py.h(), sqy.h())
                rsum('dve', sm[:, 8:16], sqy[:].rearrange("p (a b) -> p a b", a=8), sqy.h(), sm.h())
                sm2 = small()
                ts('dve', sm2[:, 0:8], sm[:, 0:8], 1.0 / 64, None, ALU.mult, None, sm.h(), sm2.h())
                tt('dve', sm2[:, 8:16], sm2[:, 0:8], sm2[:, 0:8], ALU.mult, sm2.h(), sm2.h())
                stt('dve', sm2[:, 8:16], sm[:, 8:16], 1.0 / 64, sm2[:, 8:16], ALU.mult, ALU.subtract, sm.h() + sm2.h(), sm2.h())
                rs_ = rstd_from(sm2[:, 8:16], 1.0, 64e-5, sm2.h())
                yn = tmp()
                tt('dve', yn[:].rearrange("p (a b) -> p a b", a=8), py[:].rearrange("p (a b) -> p a b", a=8),
                   sm2[:, 0:8].unsqueeze(2).broadcast_to([128, 8, 64]), ALU.subtract, py.h() + sm2.h(), yn.h())
                tt('dve', yn[:].rearrange("p (a b) -> p a b", a=8), yn[:].rearrange("p (a b) -> p a b", a=8),
                   rs_[:, 0:8].unsqueeze(2).broadcast_to([128, 8, 64]), ALU.mult, yn.h() + rs_.h(), yn.h())
                tt('pool', yn[:], yn[:], cr(11), ALU.mult, yn.h() + crow.h(), yn.h())
                tt('pool', yn[:], yn[:], cr(12), ALU.add, yn.h() + crow.h(), yn.h())
                rk = tmp()
                tt('pool', rk[:], r_tm[:], kf[:], ALU.mult, r_tm.h() + kf.h(), rk.h())
                tt('pool', rk[:], rk[:], cr(10), ALU.mult, rk.h() + crow.h(), rk.h())
                sm3 = small()
                rsum('dve', sm3[:, 0:8], rk[:].rearrange("p (a b) -> p a b", a=8), rk.h(), sm3.h())
                tt('dve', rk[:].rearrange("p (a b) -> p a b", a=8), v_tm[:].rearrange("p (a b) -> p a b", a=8),
                   sm3[:, 0:8].unsqueeze(2).broadcast_to([128, 8, 64]), ALU.mult, v_tm.h() + sm3.h(), rk.h())
                tt('pool', yn[:], yn[:], rk[:], ALU.add, yn.h() + rk.h(), yn.h())
                tt('pool', yn[:], yn[:], gate[:], ALU.mult, yn.h() + gate.h(), yn.h())
                if ti == 0 and l == 0 and j == 0:
                    dump("yd0", yn[:], yn.h())
                pt = pst()
                for cc in range(4):
                    tr(pt[:, cc * 128:(cc + 1) * 128], yn[:, cc * 128:(cc + 1) * 128], yn.h(), pt.h())
                cp('act', yT[3][:, :, js], pt[:].rearrange("p (a b) -> p a b", a=4), pt.h(), yT[3].h())

            if ti == 0 and l == 0:
                for b in range(4):
                    dump(f"yT{b}", yT[b][:].rearrange("p a b -> p (a b)"), yT[b].h())

            for jb in range(4):
                for half in range(2):
                    s_g_ = wload(b_gate[l], 0, KC, jb * D + half * 512, 512)
                    s_b_ = wload(b_br[l, jb], 0, 4, half * 512, 512)
                    for q in range(4):
                        oc = half * 4 + q
                        pg, pb = pst(), pst()
                        fm_group(pg, s_g_, q, hT, KC, hT.h(), roff=1)
                        fm_group(pb, s_b_, q, yT[jb], 4, yT[jb].h())
                        sg = tmp()
                        bc = P0 + 48 + jb * 8 + oc
                        act(sg[:, 0:T], pg[:, 0:T], AF.Sigmoid, pg.h() + cpp.h(), sg.h(), bias=cpp[:, bc:bc + 1])
                        if jb == 0:
                            tt('dve', FB[oc // 4][:, oc % 4, :], sg[:, 0:T], pb[:, 0:T], ALU.mult, sg.h() + pb.h(), FB[oc // 4].h(oc % 4))
                        else:
                            tt('dve', sg[:, 0:T], sg[:, 0:T], pb[:, 0:T], ALU.mult, sg.h() + pb.h(), sg.h())
                            if jb < 3:
                                tt('pool', FB[oc // 4][:, oc % 4, :], FB[oc // 4][:, oc % 4, :], sg[:, 0:T], ALU.add,
                                   FB[oc // 4].h(oc % 4) + sg.h(), FB[oc // 4].h(oc % 4))
                            else:
                                tt('pool', mixT[:, oc, :], FB[oc // 4][:, oc % 4, :], sg[:, 0:T], ALU.add,
                                   FB[oc // 4].h(oc % 4) + sg.h(), mixT.h(oc))
            for half in range(2):
                s_o = wload(b_out[l], 0, KC, half * 512, 512)
                for j in range(J):
                    po_ = pst()
                    tm_group(po_, mixT, j * 128, s_o, KC, mixT.h())
                    tt('dve', xres[:, j, half * 512:(half + 1) * 512], xres[:, j, half * 512:(half + 1) * 512], po_[:],
                       ALU.add, xres.h(j) + po_.h(), xres.h(j))
            if ti == 0 and l == 0:
                dump("xmid", xres[:].rearrange("p a b -> p (a b)"), xres.h())
            rmsnorm_to_hT(P0 + 8)
            for blk in range(6):
                nc_ = 512 if blk < 5 else 256
                s_fg = wload(b_fg[l], 0, KC, blk * 512, nc_)
                s_fu = wload(b_fu[l], 0, KC, blk * 512, nc_)
                for q in range(nc_ // 128):
                    hk = blk * 4 + q
                    pg, pu = pst(), pst()
                    fm_group(pg, s_fg, q, hT, KC, hT.h(), roff=1)
                    fm_group(pu, s_fu, q, hT, KC, hT.h(), roff=1)
                    sg = tmp()
                    act(sg[:, 0:T], pg[:, 0:T], AF.Silu, pg.h(), sg.h())
                    tt('dve', hidT[:, hk, :], sg[:, 0:T], pu[:, 0:T], ALU.mult, sg.h() + pu.h(), hidT.h(hk))
            for half in range(2):
                sl = [wload(b_fd[l], 0, 8, half * 512, 512), wload(b_fd[l], 8, 8, half * 512, 512),
                      wload(b_fd[l], 16, 6, half * 512, 512)]
                for j in range(J):
                    pd = pst()
                    for gi, (s_, nk) in enumerate(zip(sl, (8, 8, 6))):
                        tm_group(pd, hidT, j * 128, s_, nk, hidT.h(), first=(gi == 0), last=(gi == 2), kc0=gi * 8)
                    tt('dve', xres[:, j, half * 512:(half + 1) * 512], xres[:, j, half * 512:(half + 1) * 512], pd[:],
                       ALU.add, xres.h(j) + pd.h(), xres.h(j))
            if ti == 0 and l == 0:
                dump("xl0", xres[:].rearrange("p a b -> p (a b)"), xres.h())
        dma('sp', frow[:], frow_d.partition_broadcast(128), [], frow.h(), "sd_fr")
        for j in range(J):
            ss = small()
            jk = tmp()
            act(jk[:], xres[:, j, 0:512], AF.Square, xres.h(j), jk.h() + ss.h(), accum=ss[:, 0:1])
            act(jk[:], xres[:, j, 512:1024], AF.Square, xres.h(j), jk.h() + ss.h(), accum=ss[:, 1:2])
            tt('dve', ss[:, 0:1], ss[:, 0:1], ss[:, 1:2], ALU.add, ss.h(), ss.h())
            r = rstd_from(ss[:, 0:1], 1.0 / D, 1e-6, ss.h())
            ts('dve', xres[:, j, :], xres[:, j, :], r[:, 0:1], None, ALU.mult, None, xres.h(j) + r.h(), xres.h(j))
            tt('pool', xres[:, j, :], xres[:, j, :], frow[:], ALU.mult, xres.h(j) + frow.h(), xres.h(j))
        pass
        dma('sp', out_d[t0:t0 + T, :].rearrange("(j p) d -> p j d", p=128), xres[:], xres.h(), [("dram", "out")], "sd_o")

    fw = ["sd_o"] + ["sg_dbg_" + n for n in dbg_out]
    kb.emit(fw)
    es.close()
    return nc


def pack_inputs(inp, b, SEQ):
    f = lambda a: np.ascontiguousarray(np.asarray(a, dtype=np.float32))
    m = {}
    m["x"] = f(inp["x"][b, :SEQ])
    m["w_in"] = f(inp["w_in"])
    m["w_gate"] = f(np.transpose(inp["w_gate"], (0, 2, 1, 3)).reshape(2, D, 4 * D))
    m["w_branch"] = f(inp["w_branch"])
    m["w_out"] = f(inp["w_out"])
    m["w_fg"] = f(inp["w_ffn_gate"])
    m["w_fu"] = f(inp["w_ffn_up"])
    m["w_fd"] = f(inp["w_ffn_down"])

    def pp(v, n):
        return np.asarray(v, np.float32).reshape(n, 128).T
    cpp = np.zeros((128, 2 * LPP), np.float32)
    crow = np.zeros((2, NROW), np.float32)
    for l in range(2):
        P0 = l * LPP
        cpp[:, P0 + 0:P0 + 8] = pp(inp["norm_mix_g"][l], 8)
        cpp[:, P0 + 8:P0 + 16] = pp(inp["norm_ffn_g"][l], 8)
        for i in range(3):
            cpp[:, P0 + 16 + 8 * i:P0 + 24 + 8 * i] = pp(inp["rw_mu_lora"][l, i], 8)
        if l == 1:
            cpp[:, P0 + 40:P0 + 48] = pp(inp["rw_mu_vres"][0], 8)
        for jb in range(4):
            cpp[:, P0 + 48 + 8 * jb:P0 + 56 + 8 * jb] = pp(inp["b_gate"][l, jb], 8)
        for cc in range(4):
            cpp[:, P0 + 80 + cc * CONVW:P0 + 80 + (cc + 1) * CONVW] = np.asarray(inp["conv_w"][l])[:, cc * 128:(cc + 1) * 128].T
        cpp[:, P0 + 204:P0 + 208] = pp(inp["conv_b"][l], 4)
        cpp[:, P0 + 208:P0 + 212] = pp(inp["conv_ln_g"][l], 4)
        cpp[:, P0 + 212:P0 + 216] = pp(inp["conv_ln_b"][l], 4)
        cpp[:, P0 + 216:P0 + 220] = pp(inp["hg_norm_g"][l], 4)
        cpp[:, P0 + 220:P0 + 224] = pp(inp["hg_lb"][0], 4)
        cpp[:, P0 + 224:P0 + 228] = pp(inp["hg_lb"][1], 4)
        rows = [inp["sgu_ln_g"][l], inp["sgu_ln_b"][l], inp["rw_mu_rkv"][l, 0], inp["rw_mu_rkv"][l, 1],
                inp["rw_mu_rkv"][l, 2], inp["rw_w0"][l], inp["rw_a0"][l],
                inp["rw_v0"][0] if l == 1 else np.zeros(512, np.float32),
                inp["rw_k_k"][l], inp["rw_k_a"][l], np.asarray(inp["rw_r_k"][l]).reshape(512),
                inp["rw_ln_g"][l], inp["rw_ln_b"][l], np.asarray(inp["sgu_b"][l]).reshape(512)]
        crow[l] = np.concatenate([np.asarray(r, np.float32).reshape(512) for r in rows])
    m["cpp"] = cpp
    m["crow"] = crow
    m["frow"] = f(inp["final_norm_g"]).reshape(1, D)
    m["lbrow"] = f(inp["hg_lb"])
    m["spw"] = f(np.transpose(inp["sgu_w"], (0, 3, 1, 2)))
    l1 = np.zeros((2, D, 320), np.float32)
    l2 = np.zeros((2, 128, 5, 512), np.float32)
    for l in range(2):
        l1[l, :, 0:64] = inp["rw_w1"][l]
        l1[l, :, 64:128] = inp["rw_a1"][l]
        l1[l, :, 128:288] = inp["rw_g1"][l]
        l2[l, 0:64, 0] = inp["rw_w2"][l]
        l2[l, 0:64, 1] = inp["rw_a2"][l]
        l2[l, 0:128, 2] = np.asarray(inp["rw_g2"][l])[0:128]
        l2[l, 0:32, 3] = np.asarray(inp["rw_g2"][l])[128:160]
    l1[1, :, 288:320] = inp["rw_v1"][0]
    l2[1, 0:32, 4] = inp["rw_v2"][0]
    m["l1"] = l1
    m["l2"] = l2
    return m


_NC_CACHE = {}


def kernel(**inputs):
    x = np.asarray(inputs["x"])
    B, SEQ, _ = x.shape
    if SEQ not in _NC_CACHE:
        _NC_CACHE[SEQ] = build(SEQ)
    nc = _NC_CACHE[SEQ]
    maps = [pack_inputs(inputs, b, SEQ) for b in range(B)]
    in_maps = [maps[c % B] for c in range(8)]
    res = run_bass_kernel_spmd(nc, in_maps, core_ids=list(range(8)))
    out = np.stack([np.asarray(res.results[b]["out"], dtype=np.float32) for b in range(B)], axis=0)
    return out
```

```python
import numpy as np
from contextlib import ExitStack
import concourse.bass as bass
import concourse.mybir as mybir
from concourse.bass_utils import run_bass_kernel_spmd

F32 = mybir.dt.float32
BF16 = mybir.dt.bfloat16
AF = mybir.ActivationFunctionType
ALU = mybir.AluOpType
AX = mybir.AxisListType

D = 1024
KC = 8
T = 256
J = T // 128
INC = 5632
FFH = 2816
NROW = 14 * 512
LPP = 228
CONVW = 31
HIST = 30


class Buf:
    def __init__(self, t, name, leaves):
        self.t, self.name, self.leaves = t, name, leaves

    def __getitem__(self, idx):
        return self.t[idx]

    def h(self, *ls):
        return [(self.name, l) for l in (ls if ls else range(self.leaves))]


class KB:
    ENG = ('pe', 'act', 'dve', 'pool', 'sp')

    def __init__(self, nc, es):
        self.nc, self.es = nc, es
        self.prog = {e: [] for e in self.ENG}
        self.cnt = {}
        self.waited = {e: {} for e in self.ENG}
        self.lastw = {}
        self.readers = {}
        self.semh = {}
        self.dma_sems = set()

    def sb(self, name, shape, dt, leaves=1):
        return Buf(self.es.enter_context(self.nc.sbuf_tensor("S_" + name, list(shape), dt)), name, leaves)

    def ps(self, name, shape, dt=F32, leaves=1):
        return Buf(self.es.enter_context(self.nc.psum_tensor("P_" + name, list(shape), dt)), name, leaves)

    def op(self, eng, fn, reads=(), writes=(), dsem=None):
        need = {}

        def add(d):
            if d is not None:
                need[d[0]] = max(need.get(d[0], 0), d[1])
        for h in reads:
            add(self.lastw.get(h))
        for h in writes:
            add(self.lastw.get(h))
            for d in self.readers.get(h, {}).items():
                add(d)
        if eng == 'pe':
            need.pop('pe', None)
        waits = []
        for k, v in need.items():
            if self.waited[eng].get(k, 0) < v:
                self.waited[eng][k] = v
                waits.append((k, v))
        if dsem is None:
            self.cnt[eng] = self.cnt.get(eng, 0) + 1
            me = (eng, self.cnt[eng])
        else:
            self.dma_sems.add(dsem)
            self.cnt[dsem] = self.cnt.get(dsem, 0) + 16
            me = (dsem, self.cnt[dsem])
        self.prog[eng].append((waits, fn, me))
        for h in reads:
            r = self.readers.setdefault(h, {})
            r[me[0]] = max(r.get(me[0], 0), me[1])
        for h in writes:
            self.lastw[h] = me
            self.readers[h] = {}
        return me

    def emit(self, final_waits):
        keys = set()
        waited = {}
        for e in self.ENG:
            for waits, fn, me in self.prog[e]:
                keys.add(me[0])
                for k, v in waits:
                    keys.add(k)
                    waited.setdefault(k, set()).add(v)
        rank = {}
        for k, vs in waited.items():
            if k not in self.dma_sems:
                rank[k] = {v: i + 1 for i, v in enumerate(sorted(vs))}
        for k in sorted(keys):
            self.semh[k] = self.es.enter_context(self.nc.semaphore("s_" + k))
        block = self.es.enter_context(self.nc.Block())

        def mk(eng):
            def body(e):
                for waits, fn, me in self.prog[eng]:
                    for k, v in waits:
                        e.wait_ge(self.semh[k], v if k in self.dma_sems else rank[k][v])
                    ins = fn(e)
                    if me[0] in self.dma_sems:
                        ins.then_inc(self.semh[me[0]], 16)
                    elif me[1] in rank.get(me[0], {}):
                        ins.then_inc(self.semh[me[0]], 1)
                if eng == 'sp':
                    for k in final_waits:
                        e.wait_ge(self.semh[k], self.cnt[k])
            return body
        block.tensor(mk('pe'))
        block.scalar(mk('act'))
        block.vector(mk('dve'))
        block.gpsimd(mk('pool'))
        block.sync(mk('sp'))


def build(SEQ, dbg=None):
    NT = SEQ // T
    nc = bass.Bass("TRN2", target_bir_lowering=False)
    es = ExitStack()
    kb = KB(nc, es)
    op = kb.op

    def din(name, shape, dt=F32):
        return nc.dram_tensor(name, list(shape), dt, kind="ExternalInput").ap()

    x_d = din("x", [SEQ, D])
    out_d = nc.dram_tensor("out", [SEQ, D], F32, kind="ExternalOutput").ap()
    w_in_d = din("w_in", [2, D, INC])
    w_gate_d = din("w_gate", [2, D, 4 * D])
    w_br_d = din("w_branch", [2, 4, 512, D])
    w_out_d = din("w_out", [2, D, D])
    w_fg_d = din("w_fg", [2, D, FFH])
    w_fu_d = din("w_fu", [2, D, FFH])
    w_fd_d = din("w_fd", [2, FFH, D])
    cpp_d = din("cpp", [128, 2 * LPP])
    crow_d = din("crow", [2, NROW])
    frow_d = din("frow", [1, D])
    lbrow_d = din("lbrow", [2, 512])
    spw_d = din("spw", [2, 128, 4, 128])
    l1_d = din("l1", [2, D, 320])
    l2_d = din("l2", [2, 128, 5, 512])

    def dscr(name, shape):
        return nc.dram_tensor(name, list(shape), BF16).ap()
    b_in = dscr("b_in", [2, D, INC])
    b_gate = dscr("b_gate", [2, D, 4 * D])
    b_br = dscr("b_br", [2, 4, 512, D])
    b_out = dscr("b_out", [2, D, D])
    b_fg = dscr("b_fg", [2, D, FFH])
    b_fu = dscr("b_fu", [2, D, FFH])
    b_fd = dscr("b_fd", [2, FFH, D])
    b_l1 = dscr("b_l1", [2, D, 320])
    b_l2 = dscr("b_l2", [2, 128, 5, 512])

    dbg_out = {}
    if dbg:
        for name, shp in dbg.items():
            dbg_out[name] = nc.dram_tensor("dbg_" + name, list(shp), F32, kind="ExternalOutput").ap()

    sb, ps = kb.sb, kb.ps

    xres = sb("xres", [128, J, D], F32, J)
    hT = sb("hT", [128, KC, T + 1], BF16)
    hcar = [sb(f"hcar{l}", [128, KC], BF16) for l in range(2)]
    NSLOT = 4
    wring = [sb(f"wr{i}", [128, KC, 512], BF16) for i in range(NSLOT)]
    cpp = sb("cpp", [128, 2 * LPP], F32)
    crow = sb("crow", [128, NROW], F32)
    omlbtm = sb("omlbtm", [128, 512], F32)
    lbfm = sb("lbfm", [128, 8], F32)
    spw = [sb(f"spw{l}", [128, 4, 128], BF16) for l in range(2)]
    l1w = sb("l1w", [128, KC, 320], BF16)
    l2w = sb("l2w", [128, 5, 512], BF16)
    yT = [sb(f"yT{b}", [128, 4, T], BF16, 1) for b in range(4)]
    mixT = sb("mixT", [128, KC, T], BF16, KC)
    vfirst = sb("vfirst", [128, J, 512], F32, J)
    aT = [sb(f"aT{l}", [128, 4, HIST + T], F32, 4) for l in range(2)]
    FB = [sb(f"FB{i}", [128, 4, T], F32, 4) for i in range(3)]
    Shg = [sb(f"Shg{l}", [128, 4, 128], F32, 4) for l in range(2)]
    Srw = [sb(f"Srw{l}", [128, 4, 128], F32, 4) for l in range(2)]
    s1 = {m: sb("s1" + m, [128, T], BF16) for m in ("w", "a", "g0", "g1", "v")}
    NBIG = 12
    BIG = sb("BIG", [128, NBIG * 512], F32, NBIG)

    class View:
        def __init__(self, ap, blocks):
            self.ap, self.blocks = ap, blocks

        def __getitem__(self, idx):
            return self.ap[idx]

        def h(self, *a):
            return BIG.h(*self.blocks)

    def Dv(i):
        return View(BIG[:, i * 512:(i + 1) * 512], [i])
    Dn = [Dv(i) for i in range(NBIG)]
    frow = View(BIG[:, 0:1024], [0, 1])
    xn = View(BIG[:, 1024:2048], [2, 3])
    dhT = View(BIG[:, 2048:3072].bitcast(BF16).rearrange("p (a b) -> p a b", a=KC), [4, 5])
    xl = View(BIG[:, 3072:4096].bitcast(BF16).rearrange("p (a b) -> p a b", a=KC), [6, 7])
    lbrow = View(BIG[:, 4096:5120].rearrange("p (a b) -> p a b", a=2), [8, 9])
    hid_all = BIG[:, 0:22 * T // 2].bitcast(BF16).rearrange("p (a b) -> p a b", a=22)

    class Hid:
        def __getitem__(self, idx):
            return hid_all[idx]

        def h(self, *hks):
            if not hks:
                hks = range(22)
            return BIG.h(*sorted(set((hk * T * 2) // 2048 for hk in hks)))
    hidT = Hid()
    ident = sb("ident", [128, 128], F32)
    onesf = sb("onesf", [128, 128], F32)
    uincl = sb("uincl", [128, 128], F32)
    ublk = sb("ublk", [128, 128], F32)
    bones = sb("bones", [128, 128], F32)
    ind = sb("ind", [128, 4], F32)
    mk4 = sb("mk4", [128, 4, 128], F32)
    nlstr = sb("nlstr", [128, 128], F32)
    NTMP = 8
    tmps = [sb(f"tmp{i}", [128, 512], F32) for i in range(NTMP)]
    tmpi = [0]

    def tmp():
        b = tmps[tmpi[0] % NTMP]
        tmpi[0] += 1
        return b
    NSM = 12
    smalls = [sb(f"sm{i}", [128, 16], F32) for i in range(NSM)]
    smi = [0]

    def small():
        b = smalls[smi[0] % NSM]
        smi[0] += 1
        return b
    KR = sb("KR", [128, 4, 2, 128], F32)
    KtT = sb("KtT", [128, 4, 128], F32)
    BtT = sb("BtT", [128, 4, 128], F32)
    E1 = [sb(f"E1_{i}", [128, 4, 128], F32) for i in range(2)]
    XB = [[sb(f"XB{e}_{p}", [128, 3, 128], BF16) for p in range(2)] for e in range(2)]
    MT = [sb(f"MT{e}", [128, 128], F32) for e in range(2)]
    akv = [sb(f"akv{e}", [128, 64], F32) for e in range(2)]
    identb = sb("identb", [128, 128], BF16)
    U1p = [sb(f"U1p{i}", [128, 128], F32, 2) for i in range(2)]
    W1T = [sb(f"W1T{i}", [128, 128], F32, 2) for i in range(2)]
    Ut = [sb(f"Ut{i}", [128, 128], F32) for i in range(2)]
    NPS = 6
    psb = [ps(f"ps{i}", [128, 512], F32) for i in range(NPS)]
    psL = [ps(f"psL{i}", [128, 512], F32) for i in range(2)]
    psi = [0]

    def pst():
        b = psb[psi[0] % NPS]
        psi[0] += 1
        return b

    def dma(eng, out_ap, in_ap, reads, writes, dsem):
        return op(eng, lambda e: e.dma_start(out=out_ap, in_=in_ap), reads, writes, dsem)

    def mm(out_ap, lhsT, rhs, start, stop, reads, writes):
        return op('pe', lambda e: e.matmul(out_ap, lhsT, rhs, start=start, stop=stop), reads, writes)

    def tr(out_ap, in_ap, reads, writes):
        return op('pe', lambda e: e.transpose(out_ap, in_ap, ident[:]), reads + ident.h(), writes)

    def act(out_ap, in_ap, func, reads, writes, scale=1.0, bias=0.0, accum=None, eng='act'):
        if accum is None:
            return op(eng, lambda e: e.activation(out=out_ap, in_=in_ap, func=func, bias=bias, scale=scale), reads, writes)
        return op(eng, lambda e: e.activation(out=out_ap, in_=in_ap, func=func, bias=bias, scale=scale, accum_out=accum), reads, writes)

    def tt(eng, out_ap, a, b, alu, reads, writes):
        return op(eng, lambda e: e.tensor_tensor(out=out_ap, in0=a, in1=b, op=alu), reads, writes)

    def ts(eng, out_ap, a, s1_, s2_, op0, op1, reads, writes):
        if op1 is None:
            return op(eng, lambda e: e.tensor_scalar(out=out_ap, in0=a, scalar1=s1_, scalar2=None, op0=op0), reads, writes)
        return op(eng, lambda e: e.tensor_scalar(out=out_ap, in0=a, scalar1=s1_, scalar2=s2_, op0=op0, op1=op1), reads, writes)

    def stt(eng, out_ap, a, s, b, op0, op1, reads, writes):
        return op(eng, lambda e: e.scalar_tensor_tensor(out=out_ap, in0=a, scalar=s, in1=b, op0=op0, op1=op1), reads, writes)

    def cp(eng, out_ap, in_ap, reads, writes):
        if eng == 'act':
            return act(out_ap, in_ap, AF.Copy, reads, writes)
        return op(eng, lambda e: e.tensor_copy(out=out_ap, in_=in_ap), reads, writes)

    def rsum(eng, out_ap, in_ap, reads, writes):
        return op(eng, lambda e: e.reduce_sum(out=out_ap, in_=in_ap, axis=AX.X), reads, writes)

    def recip(out_ap, in_ap, reads, writes):
        return op('dve', lambda e: e.reciprocal(out=out_ap, in_=in_ap), reads, writes)

    def mset(eng, ap, val, writes):
        return op(eng, lambda e: e.memset(ap, val), (), writes)

    def asel(out_ap, in_ap, pattern, cmp_op, fill, base, cm, reads, writes):
        return op('pool', lambda e: e.affine_select(out=out_ap, in_=in_ap, pattern=pattern, compare_op=cmp_op,
                                                    fill=fill, base=base, channel_multiplier=cm), reads, writes)

    def dump(name, ap, reads):
        if name in dbg_out:
            dma('pool', dbg_out[name], ap, reads, [("dbg", name)], "sg_dbg_" + name)

    mset('pool', ident[:], 0.0, ident.h())
    asel(ident[:], ident[:], [[-1, 128]], ALU.not_equal, 1.0, 0, 1, ident.h(), ident.h())
    cp('pool', identb[:], ident[:], ident.h(), identb.h())
    mset('pool', onesf[:], 1.0, onesf.h())
    mset('pool', uincl[:], 1.0, uincl.h())
    asel(uincl[:], uincl[:], [[1, 128]], ALU.is_ge, 0.0, 0, -1, uincl.h(), uincl.h())
    mset('pool', bones[:], 1.0, bones.h())
    b3 = bones[:].rearrange("p (a b) -> p a b", a=4)
    asel(b3, b3, [[-32, 4], [0, 32]], ALU.is_ge, 0.0, 0, 1, bones.h(), bones.h())
    asel(b3, b3, [[32, 4], [0, 32]], ALU.is_ge, 0.0, 31, -1, bones.h(), bones.h())
    tt('pool', ublk[:], uincl[:], bones[:], ALU.mult, uincl.h() + bones.h(), ublk.h())
    mset('pool', ind[:], 1.0, ind.h())
    asel(ind[:], ind[:], [[-32, 4]], ALU.is_ge, 0.0, 0, 1, ind.h(), ind.h())
    asel(ind[:], ind[:], [[32, 4]], ALU.is_ge, 0.0, 31, -1, ind.h(), ind.h())
    mset('pool', mk4[:], 1.0, mk4.h())
    asel(mk4[:, 0, :], mk4[:, 0, :], [[1, 128]], ALU.is_ge, 0.0, -1, -1, mk4.h(), mk4.h())
    cp('pool', mk4[:, 1, :], uincl[:], uincl.h() + mk4.h(), mk4.h())
    ts('pool', mk4[:, 2, :], mk4[:, 0, :], -1.0, None, ALU.mult, None, mk4.h(), mk4.h())
    cp('pool', mk4[:, 3, :], uincl[:], uincl.h() + mk4.h(), mk4.h())
    mset('pool', nlstr[:], -1.0, nlstr.h())
    asel(nlstr[:], nlstr[:], [[-1, 128]], ALU.is_ge, 0.0, -1, 1, nlstr.h(), nlstr.h())

    dma('sp', cpp[:], cpp_d, [], cpp.h(), "sd_c")
    dma('sp', BIG[:, 4096:5120], lbrow_d.rearrange("a b -> (a b)").partition_broadcast(128),
        [], lbrow.h(), "sd_c")
    for l in range(2):
        dma('pool', spw[l][:], spw_d[l], [], spw[l].h(), "sg_c")
    for hh in spw[0].h() + spw[1].h():
        kb.lastw[hh] = ("sg_c", kb.cnt["sg_c"])
    for hh in cpp.h() + lbrow.h():
        kb.lastw[hh] = ("sd_c", kb.cnt["sd_c"])
    WH = [("dram", "w")]

    def cast(dst, src):
        dma('pool', dst, src, [], WH, "sg_w")
    for l in range(2):
        cast(b_l2[l], l2_d[l])
        for r in range(KC):
            rs = slice(r * 128, (r + 1) * 128)
            cast(b_l1[l, rs, :], l1_d[l, rs, :])
            cast(b_in[l, rs, :], w_in_d[l, rs, :])
            cast(b_gate[l, rs, :], w_gate_d[l, rs, :])
            cast(b_out[l, rs, :], w_out_d[l, rs, :])
            cast(b_fg[l, rs, :], w_fg_d[l, rs, :])
            cast(b_fu[l, rs, :], w_fu_d[l, rs, :])
        for r in range(22):
            rs = slice(r * 128, (r + 1) * 128)
            cast(b_fd[l, rs, :], w_fd_d[l, rs, :])
        for jb in range(4):
            for r in range(4):
                rs = slice(r * 128, (r + 1) * 128)
                cast(b_br[l, jb, rs, :], w_br_d[l, jb, rs, :])
    kb.lastw[WH[0]] = ("sg_w", kb.cnt["sg_w"])

    for l in range(2):
        mset('pool', spw[l][64:128, :, 0:64], 0.0, spw[l].h())
    lbtm = tmp()
    tt('dve', lbtm[:], lbrow[:, 1, :], lbrow[:, 0, :], ALU.subtract, lbrow.h(), lbtm.h())
    act(lbtm[:], lbtm[:], AF.Sigmoid, lbtm.h(), lbtm.h())
    ts('dve', omlbtm[:], lbtm[:], -1.0, 1.0, ALU.mult, ALU.add, lbtm.h(), omlbtm.h())
    tt('dve', lbfm[:, 0:4], cpp[:, 224:228], cpp[:, 220:224], ALU.subtract, cpp.h(), lbfm.h())
    act(lbfm[:, 0:4], lbfm[:, 0:4], AF.Sigmoid, lbfm.h(), lbfm.h())
    ts('dve', lbfm[:, 4:8], lbfm[:, 0:4], -1.0, 1.0, ALU.mult, ALU.add, lbfm.h(), lbfm.h())
    for l in range(2):
        mset('pool', hcar[l][:], 0.0, hcar[l].h())
        mset('pool', aT[l][:, :, 0:HIST], 0.0, aT[l].h())
        mset('pool', Shg[l][:], 0.0, Shg[l].h())
        mset('pool', Srw[l][:], 0.0, Srw[l].h())

    slot_i = [0]

    def wload(src2d, r0, nkc, c0, ncols):
        i = slot_i[0] % NSLOT
        slot_i[0] += 1
        s = wring[i]
        src = src2d[r0 * 128:(r0 + nkc) * 128, c0:c0 + ncols].rearrange("(kc p) n -> p kc n", p=128)
        dma('sp', s[:, 0:nkc, 0:ncols], src, WH, s.h(), f"sd_w{i}")
        return s

    def fm_group(pt, s, cc, rhsbuf, nkc, rd, ncol=T, roff=0):
        for kc in range(nkc):
            mm(pt[:, 0:ncol], s[:, kc, cc * 128:(cc + 1) * 128], rhsbuf[:, kc, roff:roff + ncol],
               kc == 0, kc == nkc - 1, s.h() + rd, pt.h())

    def tm_group(pt, lbuf, loff, s, nkc, rd, ncols=512, first=True, last=True, kc0=0):
        for kc in range(nkc):
            mm(pt[:, 0:ncols], lbuf[:, kc0 + kc, loff:loff + 128], s[:, kc, 0:ncols],
               first and kc == 0, last and kc == nkc - 1, s.h() + rd, pt.h())

    def rstd_from(ssum_ap, scale, eps, rd):
        n = ssum_ap.shape[-1]
        o = small()
        act(o[:, 0:n], ssum_ap, AF.Sqrt, rd, o.h(), scale=scale, bias=eps)
        recip(o[:, 0:n], o[:, 0:n], o.h(), o.h())
        return o

    def rmsnorm_to_hT(gcol0):
        for j in range(J):
            ss = small()
            act(xn[:], xres[:, j, :], AF.Square, xres.h(j), xn.h() + ss.h(), accum=ss[:, 0:1])
            r = rstd_from(ss[:, 0:1], 1.0 / D, 1e-6, ss.h())
            ts('dve', xn[:], xres[:, j, :], r[:, 0:1], None, ALU.mult, None, xres.h(j) + r.h(), xn.h())
            for half in range(2):
                pt = pst()
                for q in range(4):
                    kc = half * 4 + q
                    tr(pt[:, q * 128:(q + 1) * 128], xn[:, kc * 128:(kc + 1) * 128], xn.h(), pt.h())
                g3 = cpp[:, gcol0 + half * 4:gcol0 + half * 4 + 4].unsqueeze(2).broadcast_to([128, 4, 128])
                tt('dve', hT[:, half * 4:half * 4 + 4, 1 + j * 128:1 + (j + 1) * 128],
                   pt[:].rearrange("p (a b) -> p a b", a=4), g3, ALU.mult, pt.h() + cpp.h(), hT.h())

    for ti in range(NT):
        t0 = ti * T
        dma('sp', xres[:], x_d[t0:t0 + T, :].rearrange("(j p) d -> p j d", p=128), [], xres.h(), "sd_x")
        for l in range(2):
            first_tile = (ti == 0)
            P0 = l * LPP
            win = b_in[l]
            dma('sp', crow[:], crow_d[l].partition_broadcast(128), [], crow.h(), "sd_cr")
            dma('sp', l1w[:], b_l1[l].rearrange("(kc p) n -> p kc n", p=128), WH, l1w.h(), "sd_l1")
            dma('sp', l2w[:], b_l2[l], WH, l2w.h(), "sd_l2")

            def cr(i):
                return crow[:, i * 512:(i + 1) * 512]
            rmsnorm_to_hT(P0 + 0)
            cp('pool', hT[:, :, 0], hcar[l][:], hcar[l].h() + hT.h(), hT.h())
            cp('pool', hcar[l][:], hT[:, :, T], hT.h() + hcar[l].h(), hcar[l].h())
            tt('pool', dhT[:], hT[:, :, 0:T], hT[:, :, 1:T + 1], ALU.subtract, hT.h(), dhT.h())
            loras = {"w": (16, 0, 64, AF.Tanh), "a": (24, 64, 64, AF.Copy), "g": (32, 128, 160, AF.Sigmoid),
                     "v": (40, 288, 32, AF.Copy)}

            def lora_prep(m):
                mucol = loras[m][0]
                mu3 = cpp[:, P0 + mucol:P0 + mucol + 8].unsqueeze(2).broadcast_to([128, KC, T])
                tt('dve', xl[:], dhT[:], mu3, ALU.mult, dhT.h() + cpp.h(), xl.h())
                tt('pool', xl[:], xl[:], hT[:, :, 1:T + 1], ALU.add, xl.h() + hT.h(), xl.h())

            def lora_mm(m):
                (mucol, woff, R, fn) = loras[m]
                parts = [(m, woff, R)] if m != "g" else [("g0", woff, 128), ("g1", woff + 128, 32)]
                for (nm, wo, rr) in parts:
                    pt = pst()
                    for kc in range(KC):
                        mm(pt[0:rr, 0:T], l1w[:, kc, wo:wo + rr], xl[:, kc, :], kc == 0, kc == KC - 1,
                           l1w.h() + xl.h(), pt.h())
                    act(s1[nm][0:rr, :], pt[0:rr, 0:T], fn, pt.h(), s1[nm].h())
            lora_prep("w")
            s_val = wload(win, 0, KC, 0, 512)
            s_gate = wload(win, 0, KC, 512, 512)
            for cc in range(4):
                pv, pg = pst(), pst()
                fm_group(pv, s_val, cc, hT, KC, hT.h(), roff=1)
                fm_group(pg, s_gate, cc, hT, KC, hT.h(), roff=1)
                sg = tmp()
                act(sg[:, 0:T], pg[:, 0:T], AF.Sigmoid, pg.h(), sg.h())
                tt('dve', aT[l][:, cc, HIST:HIST + T], pv[:, 0:T], sg[:, 0:T], ALU.mult, pv.h() + sg.h(), aT[l].h(cc))
            lora_mm("w")
            lora_prep("a")
            for cc in range(4):
                eng = 'dve'
                cw0 = P0 + 80 + cc * CONVW
                ts(eng, FB[0][:, cc, :], aT[l][:, cc, 0:T], cpp[:, cw0:cw0 + 1], cpp[:, P0 + 204 + cc:P0 + 205 + cc],
                   ALU.mult, ALU.add, aT[l].h(cc) + cpp.h(), FB[0].h(cc))
            for tap in range(1, CONVW):
                for cc in range(4):
                    eng = 'dve'
                    cw0 = P0 + 80 + cc * CONVW + tap
                    stt(eng, FB[0][:, cc, :], aT[l][:, cc, tap:tap + T], cpp[:, cw0:cw0 + 1], FB[0][:, cc, :],
                        ALU.mult, ALU.add, aT[l].h(cc) + cpp.h() + FB[0].h(cc), FB[0].h(cc))
            for cc in range(4):
                cp('pool', aT[l][:, cc, 0:HIST], aT[l][:, cc, T:T + HIST], aT[l].h(cc), aT[l].h(cc))
            if ti == 0 and l == 0:
                dump("convT", FB[0][:].rearrange("p a b -> p (a b)"), FB[0].h())
            s_u = wload(win, 0, KC, 1024, 512)
            s_v = wload(win, 0, KC, 1536, 512)
            for cc in range(4):
                pu = pst()
                fm_group(pu, s_u, cc, hT, KC, hT.h(), roff=1)
                act(FB[2][:, cc, :], pu[:, 0:T], AF.Gelu, pu.h(), FB[2].h(cc))
            for j in range(J):
                pv = pst()
                tm_group(pv, hT, 1 + j * 128, s_v, KC, hT.h())
                gv = tmp()
                sm = small()
                act(gv[:], pv[:], AF.Gelu, pv.h(), gv.h())
                rsum('dve', sm[:, 0:1], gv[:], gv.h(), sm.h())
                jk = tmp()
                act(jk[:], gv[:], AF.Square, gv.h(), jk.h() + sm.h(), accum=sm[:, 1:2])
                ts('dve', sm[:, 2:3], sm[:, 0:1], 1.0 / 512, None, ALU.mult, None, sm.h(), sm.h())
                tt('dve', sm[:, 3:4], sm[:, 2:3], sm[:, 2:3], ALU.mult, sm.h(), sm.h())
                stt('dve', sm[:, 4:5], sm[:, 1:2], 1.0 / 512, sm[:, 3:4], ALU.mult, ALU.subtract, sm.h(), sm.h())
                r = rstd_from(sm[:, 4:5], 1.0, 1e-5, sm.h())
                ts('dve', gv[:], gv[:], sm[:, 2:3], r[:, 0:1], ALU.subtract, ALU.mult, gv.h() + sm.h() + r.h(), gv.h())
                tt('pool', gv[:], gv[:], cr(0), ALU.mult, gv.h() + crow.h(), gv.h())
                vtm = tmp()
                vb = vtm[:].bitcast(BF16)[:, 0:512]
                tt('dve', vb, gv[:], cr(1), ALU.add, gv.h() + crow.h(), vtm.h())
                pm = pst()
                for g in range(4):
                    mm(pm[:, g * 128:(g + 1) * 128], vb[:, g * 128:(g + 1) * 128], spw[l][:, g, :], True, True,
                       vtm.h() + spw[l].h(), pm.h())
                mb = tmp()
                tt('dve', mb[:], pm[:], cr(13), ALU.add, pm.h() + crow.h(), mb.h())
                tt('pool', yT[1][:, :, j * 128:(j + 1) * 128], mb[:].rearrange("p (a b) -> p a b", a=4),
                   FB[2][:, :, j * 128:(j + 1) * 128], ALU.mult, mb.h() + FB[2].h(), yT[1].h())

            lora_mm("a")
            lora_prep("g")
            psum_, psq_ = pst(), pst()
            for cc in range(4):
                act(FB[1][:, cc, :], FB[0][:, cc, :], AF.Square, FB[0].h(cc), FB[1].h(cc))
            for cc in range(4):
                mm(psum_[:, 0:T], onesf[:], FB[0][:, cc, :], cc == 0, cc == 3, onesf.h() + FB[0].h(cc), psum_.h())
            for cc in range(4):
                mm(psq_[:, 0:T], onesf[:], FB[1][:, cc, :], cc == 0, cc == 3, onesf.h() + FB[1].h(cc), psq_.h())
            mean = tmp()
            act(mean[:, 0:T], psum_[:, 0:T], AF.Copy, psum_.h(), mean.h(), scale=1.0 / 512)
            msq = tmp()
            tt('dve', msq[:, 0:T], mean[:, 0:T], mean[:, 0:T], ALU.mult, mean.h(), msq.h())
            var = tmp()
            stt('dve', var[:, 0:T], psq_[:, 0:T], 1.0 / 512, msq[:, 0:T], ALU.mult, ALU.subtract,
                psq_.h() + msq.h(), var.h())
            act(var[:, 0:T], var[:, 0:T], AF.Sqrt, var.h(), var.h(), bias=1e-5)
            recip(var[:, 0:T], var[:, 0:T], var.h(), var.h())
            for cc in range(4):
                t1 = tmp()
                tt('dve', t1[:, 0:T], FB[0][:, cc, :], mean[:, 0:T], ALU.subtract, FB[0].h(cc) + mean.h(), t1.h())
                tt('pool', t1[:, 0:T], t1[:, 0:T], var[:, 0:T], ALU.mult, t1.h() + var.h(), t1.h())
                act(yT[0][:, cc, :], t1[:, 0:T], AF.Silu, t1.h() + cpp.h(), yT[0].h(),
                    scale=cpp[:, P0 + 208 + cc:P0 + 209 + cc], bias=cpp[:, P0 + 212 + cc:P0 + 213 + cc])

            qT, kT, gT = FB[0], FB[1], FB[2]
            s_q = wload(win, 0, KC, 2048, 512)
            s_g = wload(win, 0, KC, 3584, 512)
            for hc in range(4):
                pq, pg = pst(), pst()
                fm_group(pq, s_q, hc, hT, KC, hT.h(), roff=1)
                fm_group(pg, s_g, hc, hT, KC, hT.h(), roff=1)
                act(qT[:, hc, :], pq[:, 0:T], AF.Silu, pq.h(), qT.h(hc))
                act(gT[:, hc, :], pg[:, 0:T], AF.Silu, pg.h(), gT.h(hc))
            lora_mm("g")
            if l == 1:
                lora_prep("v")
            s_f = wload(win, 0, KC, 2560, 512)
            s_i = wload(win, 0, KC, 3072, 512)
            for hc in range(4):
                pf = pst()
                fm_group(pf, s_f, hc, hT, KC, hT.h(), roff=1)
                act(kT[:, hc, :], pf[:, 0:T], AF.Sigmoid, pf.h(), kT.h(hc), scale=-1.0)
                if l == 1:
                    ts('dve', kT[:, hc, :], kT[:, hc, :], lbfm[:, 4 + hc:5 + hc], None, ALU.mult, None,
                       kT.h(hc) + lbfm.h(), kT.h(hc))
            if l == 1:
                lora_mm("v")
            for j in range(J):
                js = slice(j * 128, (j + 1) * 128)
                ktm, logf, vtm_, qtl, ktl, atm, btm, dd = Dn[0:8]
                khc = Dn[8:12]
                pf, pi = pst(), pst()
                tm_group(pf, hT, 1 + j * 128, s_f, KC, hT.h())
                tm_group(pi, hT, 1 + j * 128, s_i, KC, hT.h())
                act(ktm[:], pf[:], AF.Sigmoid, pf.h(), ktm.h(), scale=-1.0)
                if l == 1:
                    tt('pool', ktm[:], ktm[:], omlbtm[:], ALU.mult, ktm.h() + omlbtm.h(), ktm.h())
                act(logf[:], ktm[:], AF.Ln, ktm.h(), logf.h(), scale=-1.0, bias=1.0)
                cp('dve', vtm_[:], pi[:], pi.h(), vtm_.h())
                pbT, pbTM, pbeTM, pbeT = pst(), pst(), pst(), pst()
                for hc in range(4):
                    mm(pbT[:, hc * 128:(hc + 1) * 128], logf[:, hc * 128:(hc + 1) * 128], ublk[:], True, True,
                       logf.h() + ublk.h(), pbT.h())
                    mm(pbeT[:, hc * 4:hc * 4 + 4], logf[:, hc * 128:(hc + 1) * 128], ind[:], True, True,
                       logf.h() + ind.h(), pbeT.h())
                mm(pbTM[:], ublk[:], logf[:], True, True, logf.h() + ublk.h(), pbTM.h())
                mm(pbeTM[:], bones[:], logf[:], True, True, logf.h() + bones.h(), pbeTM.h())
                eb, enb = tmp(), tmp()
                act(eb[:], pbT[:], AF.Exp, pbT.h(), eb.h())
                act(enb[:], pbT[:], AF.Exp, pbT.h(), enb.h(), scale=-1.0)
                tt('dve', qtl[:].rearrange("p (a b) -> p a b", a=4), qT[:, :, js], eb[:].rearrange("p (a b) -> p a b", a=4),
                   ALU.mult, qT.h() + eb.h(), qtl.h())
                tt('pool', ktl[:].rearrange("p (a b) -> p a b", a=4), kT[:, :, js], enb[:].rearrange("p (a b) -> p a b", a=4),
                   ALU.mult, kT.h() + enb.h(), ktl.h())
                cp('act', btm[:], pbTM[:], pbTM.h(), btm.h())
                tt('dve', dd[:], pbeTM[:], btm[:], ALU.subtract, pbeTM.h() + btm.h(), dd.h())
                act(dd[:], dd[:], AF.Exp, dd.h(), dd.h())
                tt('dve', dd[:], dd[:], ktm[:], ALU.mult, dd.h() + ktm.h(), dd.h())
                for c in range(4):
                    act(khc[c][:], dd[:], AF.Copy, dd.h() + ind.h(), khc[c].h(), scale=ind[:, c:c + 1])
                ebend = small()
                act(ebend[:, 0:16], pbeT[:, 0:16], AF.Exp, pbeT.h(), ebend.h())
                pA = pst()
                for hc in range(4):
                    hs = slice(hc * 128, (hc + 1) * 128)
                    mm(pA[:, hs], ktl[:, hs], qtl[:, hs], True, True, ktl.h() + qtl.h(), pA.h())
                tt('dve', atm[:].rearrange("p (a b) -> p a b", a=4), pA[:].rearrange("p (a b) -> p a b", a=4),
                   ublk[:].unsqueeze(1).broadcast_to([128, 4, 128]), ALU.mult, pA.h() + ublk.h(), atm.h())
                po = psL[0]
                for c in range(4):
                    for hc in range(4):
                        hs = slice(hc * 128, (hc + 1) * 128)
                        cs = slice(hc * 128 + c * 32, hc * 128 + (c + 1) * 32)
                        mm(po[:, cs], Shg[l][:, hc, :], qtl[:, cs], True, False, Shg[l].h(hc) + qtl.h(), po.h())
                        mm(po[:, cs], vtm_[:, hs], atm[:, cs], False, True, vtm_.h() + atm.h(), po.h())
                        pS = pst()
                        mm(pS[:, 0:128], khc[c][:, hs], vtm_[:, hs], True, True, khc[c].h() + vtm_.h(), pS.h())
                        stt('dve', Shg[l][:, hc, :], Shg[l][:, hc, :], ebend[:, hc * 4 + c:hc * 4 + c + 1], pS[:, 0:128],
                            ALU.mult, ALU.add, Shg[l].h(hc) + ebend.h() + pS.h(), Shg[l].h(hc))
                sq = tmp()
                act(sq[:], po[:], AF.Square, po.h(), sq.h())
                pss = pst()
                mm(pss[:], onesf[:], sq[:], True, True, onesf.h() + sq.h(), pss.h())
                rr = tmp()
                act(rr[:], pss[:], AF.Sqrt, pss.h(), rr.h(), scale=1.0 / 128, bias=1e-6)
                recip(rr[:], rr[:], rr.h(), rr.h())
                tt('dve', rr[:], po[:], rr[:], ALU.mult, po.h() + rr.h(), rr.h())
                tt('dve', rr[:].rearrange("p (a b) -> p a b", a=4), rr[:].rearrange("p (a b) -> p a b", a=4),
                   gT[:, :, js], ALU.mult, rr.h() + gT.h(), rr.h())
                tt('pool', yT[2][:, :, js], rr[:].rearrange("p (a b) -> p a b", a=4),
                   cpp[:, P0 + 216:P0 + 220].unsqueeze(2).broadcast_to([128, 4, 128]), ALU.mult,
                   rr.h() + cpp.h(), yT[2].h())

            s_rkv = [wload(win, 0, KC, 4096 + 512 * m, 512) for m in range(3)]
            for j in range(J):
                js = slice(j * 128, (j + 1) * 128)
                r_tm, v_tm, k_tm, logw, a_tm, gate, kap, kf, beta, cw, Kp = Dn[0:11]
                Rt, Bh, Kt, Bt, Kh = k_tm, logw, a_tm, kap, cw
                for m, o in enumerate((r_tm, k_tm, v_tm)):
                    pp, pshf = pst(), pst()
                    tm_group(pp, hT, 1 + j * 128, s_rkv[m], KC, hT.h())
                    tm_group(pshf, hT, j * 128, s_rkv[m], KC, hT.h())
                    psb_ = tmp()
                    cp('act', psb_[:], pp[:], pp.h(), psb_.h())
                    tt('dve', o[:], pshf[:], psb_[:], ALU.subtract, pshf.h() + psb_.h(), o.h())
                    tt('dve', o[:], o[:], cr(2 + m), ALU.mult, o.h() + crow.h(), o.h())
                    tt('pool', o[:], o[:], psb_[:], ALU.add, o.h() + psb_.h(), o.h())
                pw = pst()
                mm(pw[:], s1["w"][0:64, js], l2w[0:64, 0, :], True, True, s1["w"].h() + l2w.h(), pw.h())
                tt('dve', logw[:], pw[:], cr(5), ALU.add, pw.h() + crow.h(), logw.h())
                act(logw[:], logw[:], AF.Sigmoid, logw.h(), logw.h())
                ts('dve', logw[:], logw[:], -0.6065306597126334, None, ALU.mult, None, logw.h(), logw.h())
                pa = pst()
                mm(pa[:], s1["a"][0:64, js], l2w[0:64, 1, :], True, True, s1["a"].h() + l2w.h(), pa.h())
                tt('dve', a_tm[:], pa[:], cr(6), ALU.add, pa.h() + crow.h(), a_tm.h())
                act(a_tm[:], a_tm[:], AF.Sigmoid, a_tm.h(), a_tm.h())
                pgt = pst()
                mm(pgt[:], s1["g0"][:, js], l2w[:, 2, :], True, False, s1["g0"].h() + l2w.h(), pgt.h())
                mm(pgt[:], s1["g1"][0:32, js], l2w[0:32, 3, :], False, True, s1["g1"].h() + l2w.h(), pgt.h())
                cp('act', gate[:], pgt[:], pgt.h(), gate.h())
                if l == 0:
                    cp('pool', vfirst[:, j, :], v_tm[:], v_tm.h(), vfirst.h(j))
                else:
                    pvv = pst()
                    mm(pvv[:], s1["v"][0:32, js], l2w[0:32, 4, :], True, True, s1["v"].h() + l2w.h(), pvv.h())
                    sv = tmp()
                    tt('dve', sv[:], pvv[:], cr(7), ALU.add, pvv.h() + crow.h(), sv.h())
                    act(sv[:], sv[:], AF.Sigmoid, sv.h(), sv.h())
                    dv = tmp()
                    tt('pool', dv[:], vfirst[:, j, :], v_tm[:], ALU.subtract, vfirst.h(j) + v_tm.h(), dv.h())
                    tt('pool', dv[:], dv[:], sv[:], ALU.mult, dv.h() + sv.h(), dv.h())
                    tt('pool', v_tm[:], v_tm[:], dv[:], ALU.add, v_tm.h() + dv.h(), v_tm.h())
                tt('pool', kap[:], k_tm[:], cr(8), ALU.mult, k_tm.h() + crow.h(), kap.h())
                sqk = tmp()
                act(sqk[:], kap[:], AF.Square, kap.h(), sqk.h())
                sm = small()
                rsum('dve', sm[:, 0:8], sqk[:].rearrange("p (a b) -> p a b", a=8), sqk.h(), sm.h())
                ts('dve', sm[:, 0:8], sm[:, 0:8], 1e-24, None, ALU.max, None, sm.h(), sm.h())
                rn = rstd_from(sm[:, 0:8], 1.0, 0.0, sm.h())
                tt('dve', kap[:].rearrange("p (a b) -> p a b", a=8), kap[:].rearrange("p (a b) -> p a b", a=8),
                   rn[:, 0:8].unsqueeze(2).broadcast_to([128, 8, 64]), ALU.mult, kap.h() + rn.h(), kap.h())
                stt('dve', kf[:], a_tm[:], -1.0, cr(9), ALU.add, ALU.mult, a_tm.h() + crow.h(), kf.h())
                stt('dve', kf[:], kf[:], 1.0, k_tm[:], ALU.add, ALU.mult, kf.h() + k_tm.h(), kf.h())
                tt('pool', beta[:], kap[:], a_tm[:], ALU.mult, kap.h() + a_tm.h(), beta.h())
                pcw, pcwC, pgc = pst(), pst(), pst()
                mm(pcw[:], uincl[:], logw[:], True, True, uincl.h() + logw.h(), pcw.h())
                mm(pcwC[:], onesf[:], logw[:], True, True, onesf.h() + logw.h(), pcwC.h())
                for hp in range(4):
                    mm(pgc[:, hp:hp + 1], logw[:, hp * 128:(hp + 1) * 128], onesf[:, 0:1], True, True,
                       logw.h() + onesf.h(), pgc.h())
                gcT = small()
                act(gcT[:, 0:4], pgc[:, 0:4], AF.Exp, pgc.h(), gcT.h())
                cp('act', cw[:], pcw[:], pcw.h(), cw.h())
                G = tmp()
                act(G[:], cw[:], AF.Exp, cw.h(), G.h())
                tt('dve', Rt[:], r_tm[:], G[:], ALU.mult, r_tm.h() + G.h(), Rt.h())
                Gx = tmp()
                tt('dve', Gx[:], cw[:], logw[:], ALU.subtract, cw.h() + logw.h(), Gx.h())
                act(Gx[:], Gx[:], AF.Exp, Gx.h(), Gx.h())
                tt('pool', Kp[:], kap[:], Gx[:], ALU.mult, kap.h() + Gx.h(), Kp.h())
                Gi = tmp()
                act(Gi[:], cw[:], AF.Exp, cw.h(), Gi.h(), scale=-1.0)
                tt('dve', Kt[:], kf[:], Gi[:], ALU.mult, kf.h() + Gi.h(), Kt.h())
                tt('pool', Bt[:], beta[:], Gi[:], ALU.mult, beta.h() + Gi.h(), Bt.h())
                GCr = tmp()
                tt('dve', GCr[:], pcwC[:], cw[:], ALU.subtract, pcwC.h() + cw.h(), GCr.h())
                act(GCr[:], GCr[:], AF.Exp, GCr.h(), GCr.h())
                tt('pool', Bh[:], beta[:], GCr[:], ALU.mult, beta.h() + GCr.h(), Bh.h())
                tt('pool', Kh[:], kf[:], GCr[:], ALU.mult, kf.h() + GCr.h(), Kh.h())
                for ii, (src, dst3, dh_) in enumerate(((Kp, KR[:, :, 0, :], KR), (Rt, KR[:, :, 1, :], KR),
                                                       (Kt, KtT[:], KtT), (Bt, BtT[:], BtT))):
                    pt = pst()
                    for hp in range(4):
                        tr(pt[:, hp * 128:(hp + 1) * 128], src[:, hp * 128:(hp + 1) * 128], src.h(), pt.h())
                    cp('act' if ii % 2 == 0 else 'dve', dst3, pt[:].rearrange("p (a b) -> p a b", a=4), pt.h(), dh_.h())
                py = psL[1]
                for hp in range(4):
                    for e_ in range(2):
                        rows = slice(e_ * 64, (e_ + 1) * 64)
                        p12, p3 = pst(), pst()
                        KRh = KR[rows, hp, :, :].rearrange("p a b -> p (a b)")
                        mm(p12[:, 0:256], KtT[rows, hp, :], KRh, True, True, KtT.h() + KR.h(), p12.h())
                        mm(p12[:, 256:512], BtT[rows, hp, :], KRh, True, True, BtT.h() + KR.h(), p12.h())
                        mm(p3[:, 0:128], KR[rows, hp, 0, :], BtT[rows, hp, :], True, True, KR.h() + BtT.h(), p3.h())
                        e1 = E1[e_]
                        tt('dve', e1[:], p12[:].rearrange("p (a b) -> p a b", a=4), mk4[:], ALU.mult,
                           p12.h() + mk4.h(), e1.h())
                        x0 = XB[e_][0]
                        tt('dve', x0[:, 0, :], p3[:, 0:128], nlstr[:], ALU.mult, p3.h() + nlstr.h(), x0.h())
                        cp('pool', x0[:, 1, :], e1[:, 2, :], e1.h() + x0.h(), x0.h())
                        cp('pool', x0[:, 2, :], identb[:], identb.h() + x0.h(), x0.h())
                    cur = 0
                    for s_ in range(1, 7):
                        for e_ in range(2):
                            c_, n_ = XB[e_][cur], XB[e_][1 - cur]
                            pq_ = pst()
                            mm(pq_[:, 0:128], c_[:, 1, :], c_[:, 0, :], True, True, c_.h(), pq_.h())
                            mm(pq_[:, 128:256], c_[:, 0, :], c_[:, 1, :], True, True, c_.h(), pq_.h())
                            mm(pq_[:, 256:384], identb[:], c_[:, 2, :], True, False, c_.h() + identb.h(), pq_.h())
                            mm(pq_[:, 256:384], c_[:, 0, :], c_[:, 2, :], False, True, c_.h(), pq_.h())
                            cp('act' if e_ == 0 else 'dve', n_[:].rearrange("p a b -> p (a b)"), pq_[:, 0:384],
                               pq_.h(), n_.h())
                        cur = 1 - cur
                    for e_ in range(2):
                        c_ = XB[e_][cur]
                        pq_ = pst()
                        mm(pq_[:, 0:128], identb[:], c_[:, 2, :], True, False, c_.h() + identb.h(), pq_.h())
                        mm(pq_[:, 0:128], c_[:, 0, :], c_[:, 2, :], False, True, c_.h(), pq_.h())
                        cp('act' if e_ == 0 else 'dve', MT[e_][:], pq_[:, 0:128], pq_.h(), MT[e_].h())
                    for e_ in range(2):
                        h = hp * 2 + e_
                        rows = slice(e_ * 64, (e_ + 1) * 64)
                        hcol = slice(h * 64, (h + 1) * 64)
                        e1, mt, ak = E1[e_], MT[e_], akv[e_]
                        pk = pst()
                        mm(pk[:, 0:64], e1[:, 0, :], v_tm[:, hcol], True, True, e1.h() + v_tm.h(), pk.h())
                        cp('dve', ak[:], pk[:, 0:64], pk.h(), ak.h())
                        mm(pk[:, 128:192], mt[:], ak[:], True, True, mt.h() + ak.h(), pk.h())
                        u1 = U1p[hp % 2]
                        cp('act', u1[:, rows.start:rows.stop], pk[:, 128:192], pk.h(), u1.h(e_))
                        mm(pk[:, 256:384], Kp[:, hp * 128:(hp + 1) * 128], mt[:], True, True, Kp.h() + mt.h(), pk.h())
                        w1 = W1T[hp % 2]
                        cp('dve', w1[rows, :], pk[rows, 256:384], pk.h(), w1.h(e_))
                    u1, w1, ut = U1p[hp % 2], W1T[hp % 2], Ut[hp % 2]
                    pu = pst()
                    mm(pu[:, 0:128], w1[:], Srw[l][:, hp, :], True, True, w1.h() + Srw[l].h(hp), pu.h())
                    stt('dve', ut[:], pu[:, 0:128], -1.0, u1[:], ALU.mult, ALU.subtract, pu.h() + u1.h(), ut.h())
                    for e_ in range(2):
                        h = hp * 2 + e_
                        hcol = slice(h * 64, (h + 1) * 64)
                        ecol = slice(e_ * 64, (e_ + 1) * 64)
                        e1 = E1[e_]
                        mm(py[:, hcol], KR[:, hp, 1, :], Srw[l][:, hp, ecol], True, False, KR.h() + Srw[l].h(hp), py.h())
                        mm(py[:, hcol], e1[:, 3, :], ut[:, ecol], False, False, e1.h() + ut.h(), py.h())
                        mm(py[:, hcol], e1[:, 1, :], v_tm[:, hcol], False, True, e1.h() + v_tm.h(), py.h())
                    pS = pst()
                    hps = slice(hp * 128, (hp + 1) * 128)
                    mm(pS[:, 0:128], Bh[:, hps], ut[:], True, False, Bh.h() + ut.h(), pS.h())
                    mm(pS[:, 0:128], Kh[:, hps], v_tm[:, hps], False, True, Kh.h() + v_tm.h(), pS.h())
                    for e_ in range(2):
                        rows = slice(e_ * 64, (e_ + 1) * 64)
                        stt('dve', Srw[l][rows, hp, rows], Srw[l][rows, hp, rows], gcT[rows, hp:hp + 1], pS[rows, rows],
                            ALU.mult, ALU.add, Srw[l].h(hp) + gcT.h() + pS.h(), Srw[l].h(hp))
                sm = small()
                rsum('dve', sm[:, 0:8], py[:].rearrange("p (a b) -> p a b", a=8), py.h(), sm.h())
                sqy = tmp()
                act(sqy[:], py[:], AF.Square, py.h(), sqy.h())
                rsum('dve', sm[:, 8:16], sqy[:].rearrange("p (a b) -> p a b", a=8), sqy.h(), sm.h())
                sm2 = small()
                ts('dve', sm2[:, 0:8], sm[:, 0:8], 1.0 / 64, None, ALU.mult, None, sm.h(), sm2.h())
                tt('dve', sm2[:, 8:16], sm2[:, 0:8], sm2[:, 0:8], ALU.mult, sm2.h(), sm2.h())
                stt('dve', sm2[:, 8:16], sm[:, 8:16], 1.0 / 64, sm2[:, 8:16], ALU.mult, ALU.subtract, sm.h() + sm2.h(), sm2.h())
                rs_ = rstd_from(sm2[:, 8:16], 1.0, 64e-5, sm2.h())
                yn = tmp()
                tt('dve', yn[:].rearrange("p (a b) -> p a b", a=8), py[:].rearrange("p (a b) -> p a b", a=8),
                   sm2[:, 0:8].unsqueeze(2).broadcast_to([128, 8, 64]), ALU.subtract, py.h() + sm2.h(), yn.h())
                tt('dve', yn[:].rearrange("p (a b) -> p a b", a=8), yn[:].rearrange("p (a b) -> p a b", a=8),
                   rs_[:, 0:8].unsqueeze(2).broadcast_to([128, 8, 64]), ALU.mult, yn.h() + rs_.h(), yn.h())
                tt('dve', yn[:], yn[:], cr(11), ALU.mult, yn.h() + crow.h(), yn.h())
                tt('pool', yn[:], yn[:], cr(12), ALU.add, yn.h() + crow.h(), yn.h())
                rk = tmp()
                tt('pool', rk[:], r_tm[:], kf[:], ALU.mult, r_tm.h() + kf.h(), rk.h())
                tt('pool', rk[:], rk[:], cr(10), ALU.mult, rk.h() + crow.h(), rk.h())
                sm3 = small()
                rsum('dve', sm3[:, 0:8], rk[:].rearrange("p (a b) -> p a b", a=8), rk.h(), sm3.h())
                tt('dve', rk[:].rearrange("p (a b) -> p a b", a=8), v_tm[:].rearrange("p (a b) -> p a b", a=8),
                   sm3[:, 0:8].unsqueeze(2).broadcast_to([128, 8, 64]), ALU.mult, v_tm.h() + sm3.h(), rk.h())
                tt('dve', yn[:], yn[:], rk[:], ALU.add, yn.h() + rk.h(), yn.h())
                tt('pool', yn[:], yn[:], gate[:], ALU.mult, yn.h() + gate.h(), yn.h())
                if ti == 0 and l == 0 and j == 0:
                    dump("yd0", yn[:], yn.h())
                pt = pst()
                for cc in range(4):
                    tr(pt[:, cc * 128:(cc + 1) * 128], yn[:, cc * 128:(cc + 1) * 128], yn.h(), pt.h())
                cp('act', yT[3][:, :, js], pt[:].rearrange("p (a b) -> p a b", a=4), pt.h(), yT[3].h())

            if ti == 0 and l == 0:
                for b in range(4):
                    dump(f"yT{b}", yT[b][:].rearrange("p a b -> p (a b)"), yT[b].h())

            for jb in range(4):
                for half in range(2):
                    s_g_ = wload(b_gate[l], 0, KC, jb * D + half * 512, 512)
                    s_b_ = wload(b_br[l, jb], 0, 4, half * 512, 512)
                    for q in range(4):
                        oc = half * 4 + q
                        pg, pb = pst(), pst()
                        fm_group(pg, s_g_, q, hT, KC, hT.h(), roff=1)
                        fm_group(pb, s_b_, q, yT[jb], 4, yT[jb].h())
                        sg = tmp()
                        bc = P0 + 48 + jb * 8 + oc
                        act(sg[:, 0:T], pg[:, 0:T], AF.Sigmoid, pg.h() + cpp.h(), sg.h(), bias=cpp[:, bc:bc + 1])
                        if jb == 0:
                            tt('dve', FB[oc // 4][:, oc % 4, :], sg[:, 0:T], pb[:, 0:T], ALU.mult, sg.h() + pb.h(), FB[oc // 4].h(oc % 4))
                        else:
                            tt('dve', sg[:, 0:T], sg[:, 0:T], pb[:, 0:T], ALU.mult, sg.h() + pb.h(), sg.h())
                            if jb < 3:
                                tt('pool', FB[oc // 4][:, oc % 4, :], FB[oc // 4][:, oc % 4, :], sg[:, 0:T], ALU.add,
                                   FB[oc // 4].h(oc % 4) + sg.h(), FB[oc // 4].h(oc % 4))
                            else:
                                tt('pool', mixT[:, oc, :], FB[oc // 4][:, oc % 4, :], sg[:, 0:T], ALU.add,
                                   FB[oc // 4].h(oc % 4) + sg.h(), mixT.h(oc))
            for half in range(2):
                s_o = wload(b_out[l], 0, KC, half * 512, 512)
                for j in range(J):
                    po_ = pst()
                    tm_group(po_, mixT, j * 128, s_o, KC, mixT.h())
                    tt('dve', xres[:, j, half * 512:(half + 1) * 512], xres[:, j, half * 512:(half + 1) * 512], po_[:],
                       ALU.add, xres.h(j) + po_.h(), xres.h(j))
            if ti == 0 and l == 0:
                dump("xmid", xres[:].rearrange("p a b -> p (a b)"), xres.h())
            rmsnorm_to_hT(P0 + 8)
            for blk in range(6):
                nc_ = 512 if blk < 5 else 256
                s_fg = wload(b_fg[l], 0, KC, blk * 512, nc_)
                s_fu = wload(b_fu[l], 0, KC, blk * 512, nc_)
                for q in range(nc_ // 128):
                    hk = blk * 4 + q
                    pg, pu = pst(), pst()
                    fm_group(pg, s_fg, q, hT, KC, hT.h(), roff=1)
                    fm_group(pu, s_fu, q, hT, KC, hT.h(), roff=1)
                    sg = tmp()
                    act(sg[:, 0:T], pg[:, 0:T], AF.Silu, pg.h(), sg.h())
                    tt('dve', hidT[:, hk, :], sg[:, 0:T], pu[:, 0:T], ALU.mult, sg.h() + pu.h(), hidT.h(hk))
            for half in range(2):
                sl = [wload(b_fd[l], 0, 8, half * 512, 512), wload(b_fd[l], 8, 8, half * 512, 512),
                      wload(b_fd[l], 16, 6, half * 512, 512)]
                for j in range(J):
                    pd = pst()
                    for gi, (s_, nk) in enumerate(zip(sl, (8, 8, 6))):
                        tm_group(pd, hidT, j * 128, s_, nk, hidT.h(), first=(gi == 0), last=(gi == 2), kc0=gi * 8)
                    tt('dve', xres[:, j, half * 512:(half + 1) * 512], xres[:, j, half * 512:(half + 1) * 512], pd[:],
                       ALU.add, xres.h(j) + pd.h(), xres.h(j))
            if ti == 0 and l == 0:
                dump("xl0", xres[:].rearrange("p a b -> p (a b)"), xres.h())
        dma('sp', frow[:], frow_d.partition_broadcast(128), [], frow.h(), "sd_fr")
        for j in range(J):
            ss = small()
            jk = tmp()
            act(jk[:], xres[:, j, 0:512], AF.Square, xres.h(j), jk.h() + ss.h(), accum=ss[:, 0:1])
            act(jk[:], xres[:, j, 512:1024], AF.Square, xres.h(j), jk.h() + ss.h(), accum=ss[:, 1:2])
            tt('dve', ss[:, 0:1], ss[:, 0:1], ss[:, 1:2], ALU.add, ss.h(), ss.h())
            r = rstd_from(ss[:, 0:1], 1.0 / D, 1e-6, ss.h())
            ts('dve', xres[:, j, :], xres[:, j, :], r[:, 0:1], None, ALU.mult, None, xres.h(j) + r.h(), xres.h(j))
            tt('dve', xres[:, j, :], xres[:, j, :], frow[:], ALU.mult, xres.h(j) + frow.h(), xres.h(j))
        pass
        dma('sp', out_d[t0:t0 + T, :].rearrange("(j p) d -> p j d", p=128), xres[:], xres.h(), [("dram", "out")], "sd_o")

    fw = ["sd_o"] + ["sg_dbg_" + n for n in dbg_out]
    kb.emit(fw)
    es.close()
    return nc


def pack_inputs(inp, b, SEQ):
    f = lambda a: np.ascontiguousarray(np.asarray(a, dtype=np.float32))
    m = {}
    m["x"] = f(inp["x"][b, :SEQ])
    m["w_in"] = f(inp["w_in"])
    m["w_gate"] = f(np.transpose(inp["w_gate"], (0, 2, 1, 3)).reshape(2, D, 4 * D))
    m["w_branch"] = f(inp["w_branch"])
    m["w_out"] = f(inp["w_out"])
    m["w_fg"] = f(inp["w_ffn_gate"])
    m["w_fu"] = f(inp["w_ffn_up"])
    m["w_fd"] = f(inp["w_ffn_down"])

    def pp(v, n):
        return np.asarray(v, np.float32).reshape(n, 128).T
    cpp = np.zeros((128, 2 * LPP), np.float32)
    crow = np.zeros((2, NROW), np.float32)
    for l in range(2):
        P0 = l * LPP
        cpp[:, P0 + 0:P0 + 8] = pp(inp["norm_mix_g"][l], 8)
        cpp[:, P0 + 8:P0 + 16] = pp(inp["norm_ffn_g"][l], 8)
        for i in range(3):
            cpp[:, P0 + 16 + 8 * i:P0 + 24 + 8 * i] = pp(inp["rw_mu_lora"][l, i], 8)
        if l == 1:
            cpp[:, P0 + 40:P0 + 48] = pp(inp["rw_mu_vres"][0], 8)
        for jb in range(4):
            cpp[:, P0 + 48 + 8 * jb:P0 + 56 + 8 * jb] = pp(inp["b_gate"][l, jb], 8)
        for cc in range(4):
            cpp[:, P0 + 80 + cc * CONVW:P0 + 80 + (cc + 1) * CONVW] = np.asarray(inp["conv_w"][l])[:, cc * 128:(cc + 1) * 128].T
        cpp[:, P0 + 204:P0 + 208] = pp(inp["conv_b"][l], 4)
        cpp[:, P0 + 208:P0 + 212] = pp(inp["conv_ln_g"][l], 4)
        cpp[:, P0 + 212:P0 + 216] = pp(inp["conv_ln_b"][l], 4)
        cpp[:, P0 + 216:P0 + 220] = pp(inp["hg_norm_g"][l], 4)
        cpp[:, P0 + 220:P0 + 224] = pp(inp["hg_lb"][0], 4)
        cpp[:, P0 + 224:P0 + 228] = pp(inp["hg_lb"][1], 4)
        rows = [inp["sgu_ln_g"][l], inp["sgu_ln_b"][l], inp["rw_mu_rkv"][l, 0], inp["rw_mu_rkv"][l, 1],
                inp["rw_mu_rkv"][l, 2], inp["rw_w0"][l], inp["rw_a0"][l],
                inp["rw_v0"][0] if l == 1 else np.zeros(512, np.float32),
                inp["rw_k_k"][l], inp["rw_k_a"][l], np.asarray(inp["rw_r_k"][l]).reshape(512),
                inp["rw_ln_g"][l], inp["rw_ln_b"][l], np.asarray(inp["sgu_b"][l]).reshape(512)]
        crow[l] = np.concatenate([np.asarray(r, np.float32).reshape(512) for r in rows])
    m["cpp"] = cpp
    m["crow"] = crow
    m["frow"] = f(inp["final_norm_g"]).reshape(1, D)
    m["lbrow"] = f(inp["hg_lb"])
    m["spw"] = f(np.transpose(inp["sgu_w"], (0, 3, 1, 2)))
    l1 = np.zeros((2, D, 320), np.float32)
    l2 = np.zeros((2, 128, 5, 512), np.float32)
    for l in range(2):
        l1[l, :, 0:64] = inp["rw_w1"][l]
        l1[l, :, 64:128] = inp["rw_a1"][l]
        l1[l, :, 128:288] = inp["rw_g1"][l]
        l2[l, 0:64, 0] = inp["rw_w2"][l]
        l2[l, 0:64, 1] = inp["rw_a2"][l]
        l2[l, 0:128, 2] = np.asarray(inp["rw_g2"][l])[0:128]
        l2[l, 0:32, 3] = np.asarray(inp["rw_g2"][l])[128:160]
    l1[1, :, 288:320] = inp["rw_v1"][0]
    l2[1, 0:32, 4] = inp["rw_v2"][0]
    m["l1"] = l1
    m["l2"] = l2
    return m


_NC_CACHE = {}


def kernel(**inputs):
    x = np.asarray(inputs["x"])
    B, SEQ, _ = x.shape
    if SEQ not in _NC_CACHE:
        _NC_CACHE[SEQ] = build(SEQ)
    nc = _NC_CACHE[SEQ]
    maps = [pack_inputs(inputs, b, SEQ) for b in range(B)]
    in_maps = [maps[c % B] for c in range(8)]
    res = run_bass_kernel_spmd(nc, in_maps, core_ids=list(range(8)))
    out = np.stack([np.asarray(res.results[b]["out"], dtype=np.float32) for b in range(B)], axis=0)
    return out
```
